# Optimizing a Trainium2 kernel written in Bass

```python
import jax, jax.numpy as jnp
from jax import lax
import numpy as np

D_MODEL = 1024
BATCH = 8
SEQ = 2048
DEPTH = 1
DEC_BATCH = 128
DEC_SEQ = 4
PAST_LEN = 16384
PAGE_SIZE = 128

PLE_DIM = 256
D_SG = D_MODEL
SG_HEADS = 4
SG_HEAD_DIM = D_SG // SG_HEADS
CHUNK = 128
D_LRU = D_MODEL
LRU_BLOCKS = 8
LRU_BLOCK_DIM = D_LRU // LRU_BLOCKS
CONV_WIDTH = 4
LRU_C = 8.0
EPS = 1e-6
D_IN = 3 * D_SG + 2 * D_LRU

kernel_name = 'hybrid_sgmlp_rglru_decoder_step'


def rms_norm(x, g):
    xf = x.astype(jnp.float32)
    var = jnp.mean(xf * xf, axis=-1, keepdims=True)
    return (xf * lax.rsqrt(var + EPS) * g.astype(jnp.float32)).astype(x.dtype)


def chunk_spatial_mix(z, w_s, b_s):
    B, T = z.shape[0], z.shape[1]
    n_chunks = -(-T // CHUNK)
    pad = n_chunks * CHUNK - T
    zp = jnp.pad(z, ((0, 0), (0, pad), (0, 0), (0, 0)))
    zc = zp.reshape(B, n_chunks, CHUNK, SG_HEADS, SG_HEAD_DIM)
    mask = jnp.tril(jnp.ones((CHUNK, CHUNK), dtype=bool))
    w = jnp.where(mask[None], w_s, jnp.zeros_like(w_s)).astype(z.dtype)
    s = jnp.einsum('hts,bnshd->bnthd', w, zc) + b_s.T.astype(z.dtype)[None, None, :, :, None]
    return s.reshape(B, n_chunks * CHUNK, SG_HEADS, SG_HEAD_DIM)[:, :T]


def causal_conv(xb, buf, w, b):
    T = xb.shape[1]
    xp = jnp.concatenate([buf.astype(xb.dtype), xb], axis=1)
    y = b.astype(xb.dtype)
    for k in range(CONV_WIDTH):
        y = y + w[k].astype(xb.dtype) * xp[:, k:k + T]
    return y, xp[:, -(CONV_WIDTH - 1):]


def rg_lru(xc, h0, wa, ba, wx, bx, lam):
    B, T = xc.shape[0], xc.shape[1]
    xb = xc.reshape(B, T, LRU_BLOCKS, LRU_BLOCK_DIM)
    r = jax.nn.sigmoid(jnp.einsum('btnc,ncd->btnd', xb, wa).reshape(B, T, D_LRU) + ba)
    i = jax.nn.sigmoid(jnp.einsum('btnc,ncd->btnd', xb, wx).reshape(B, T, D_LRU) + bx)
    log_a = (-LRU_C * jax.nn.softplus(-lam.astype(jnp.float32))) * r.astype(jnp.float32)
    a = jnp.exp(log_a)
    mult = jnp.sqrt(-jnp.expm1(2.0 * log_a))
    bterm = mult * (i * xc).astype(jnp.float32)

    def combine(c1, c2):
        a1, b1 = c1
        a2, b2 = c2
        return a1 * a2, a2 * b1 + b2

    a_cum, h_zero = lax.associative_scan(combine, (a, bterm), axis=1)
    h = h_zero + a_cum * h0.astype(jnp.float32)[:, None]
    return h.astype(xc.dtype), h[:, -1].astype(h0.dtype)


def mixer_layer(x, p_l, conv_buf, h0, norm_pre, w_in, sg_norm, sg_w, sg_b, conv_w, conv_b,
                lru_wa, lru_ba, lru_wx, lru_bx, lru_lambda, w_branch_sg, w_branch_lru,
                w_merge, b_merge, w_out, norm_post, w_ple, w_ple_gate, b_ple_gate):
    B, T = x.shape[0], x.shape[1]
    h = rms_norm(x, norm_pre)
    proj = h @ w_in
    u, v, g_sg, x_lru, g_lru = jnp.split(
        proj, [D_SG, 2 * D_SG, 3 * D_SG, 3 * D_SG + D_LRU], axis=-1)
    z = rms_norm(v.reshape(B, T, SG_HEADS, SG_HEAD_DIM), sg_norm.reshape(SG_HEADS, SG_HEAD_DIM))
    s = chunk_spatial_mix(z, sg_w, sg_b).reshape(B, T, D_SG)
    y_sg = u * s * jax.nn.silu(g_sg)
    xc, new_conv = causal_conv(x_lru, conv_buf, conv_w, conv_b)
    hseq, new_h = rg_lru(xc, h0, lru_wa, lru_ba, lru_wx, lru_bx, lru_lambda)
    y_lru = hseq * jax.nn.silu(g_lru)
    g_a, g_b = jnp.split(jax.nn.sigmoid(h @ w_merge + b_merge), 2, axis=-1)
    merged = g_a * (y_sg @ w_branch_sg) + g_b * (y_lru @ w_branch_lru)
    x = x + rms_norm(merged @ w_out, norm_post)
    x = x + jax.nn.sigmoid(x @ w_ple_gate + b_ple_gate) * (p_l @ w_ple)
    return x, new_conv, new_h, z.reshape(B, T, D_SG)


def setup_inputs(seed: int = 0) -> dict:
    key = jax.random.key(seed)
    ks = jax.random.split(key, 32)
    f32 = jnp.float32

    def nrm(k, shape, scale):
        return jax.random.normal(k, shape, f32) * scale

    a0 = jax.random.uniform(ks[20], (DEPTH, D_LRU), f32, 0.9, 0.999)
    s0 = a0 ** (1.0 / LRU_C)
    lru_lambda = jnp.log(s0) - jnp.log1p(-s0)
    return {
        'x_prompt': nrm(ks[0], (BATCH, SEQ, D_MODEL), 1.0),
        'x_sample': nrm(ks[1], (DEC_BATCH, DEC_SEQ, D_MODEL), 1.0),
        'p_prompt': nrm(ks[2], (DEPTH, BATCH, SEQ, PLE_DIM), 1.0),
        'p_sample': nrm(ks[3], (DEPTH, DEC_BATCH, DEC_SEQ, PLE_DIM), 1.0),
        'state_conv': nrm(ks[4], (DEPTH, DEC_BATCH, CONV_WIDTH - 1, D_LRU), 1.0),
        'state_lru': nrm(ks[5], (DEPTH, DEC_BATCH, D_LRU), 0.5),
        'norm_pre': 1.0 + nrm(ks[6], (DEPTH, D_MODEL), 0.05),
        'w_in': nrm(ks[7], (DEPTH, D_MODEL, D_IN), D_MODEL ** -0.5),
        'sg_norm': 1.0 + nrm(ks[8], (DEPTH, D_SG), 0.05),
        'sg_w': nrm(ks[9], (DEPTH, SG_HEADS, CHUNK, CHUNK), 0.5 * CHUNK ** -0.5),
        'sg_b': 1.0 + nrm(ks[10], (DEPTH, SG_HEADS, CHUNK), 0.05),
        'conv_w': nrm(ks[11], (DEPTH, CONV_WIDTH, D_LRU), CONV_WIDTH ** -0.5),
        'conv_b': nrm(ks[12], (DEPTH, D_LRU), 0.02),
        'lru_wa': nrm(ks[13], (DEPTH, LRU_BLOCKS, LRU_BLOCK_DIM, LRU_BLOCK_DIM), LRU_BLOCK_DIM ** -0.5),
        'lru_ba': nrm(ks[14], (DEPTH, D_LRU), 0.02),
        'lru_wx': nrm(ks[15], (DEPTH, LRU_BLOCKS, LRU_BLOCK_DIM, LRU_BLOCK_DIM), LRU_BLOCK_DIM ** -0.5),
        'lru_bx': nrm(ks[16], (DEPTH, D_LRU), 0.02),
        'lru_lambda': lru_lambda,
        'w_branch_sg': nrm(ks[17], (DEPTH, D_SG, D_MODEL), D_SG ** -0.5),
        'w_branch_lru': nrm(ks[18], (DEPTH, D_LRU, D_MODEL), D_LRU ** -0.5),
        'w_merge': nrm(ks[19], (DEPTH, D_MODEL, 2 * D_MODEL), D_MODEL ** -0.5),
        'b_merge': nrm(ks[21], (DEPTH, 2 * D_MODEL), 0.02),
        'w_out': nrm(ks[22], (DEPTH, D_MODEL, D_MODEL), D_MODEL ** -0.5),
        'norm_post': 1.0 + nrm(ks[23], (DEPTH, D_MODEL), 0.05),
        'w_ple': nrm(ks[24], (DEPTH, PLE_DIM, D_MODEL), PLE_DIM ** -0.5),
        'w_ple_gate': nrm(ks[25], (DEPTH, D_MODEL, D_MODEL), D_MODEL ** -0.5),
        'b_ple_gate': nrm(ks[26], (DEPTH, D_MODEL), 0.02),
    }


def reference(x_prompt, x_sample, p_prompt, p_sample, state_conv, state_lru,
              norm_pre, w_in, sg_norm, sg_w, sg_b, conv_w, conv_b,
              lru_wa, lru_ba, lru_wx, lru_bx, lru_lambda, w_branch_sg, w_branch_lru,
              w_merge, b_merge, w_out, norm_post, w_ple, w_ple_gate, b_ple_gate):
    xp = x_prompt
    xs = x_sample
    conv_p, lru_p, conv_s, lru_s, chunk_s = [], [], [], [], []
    for l in range(DEPTH):
        lw = (norm_pre[l], w_in[l], sg_norm[l], sg_w[l], sg_b[l], conv_w[l], conv_b[l],
              lru_wa[l], lru_ba[l], lru_wx[l], lru_bx[l], lru_lambda[l],
              w_branch_sg[l], w_branch_lru[l], w_merge[l], b_merge[l], w_out[l],
              norm_post[l], w_ple[l], w_ple_gate[l], b_ple_gate[l])
        buf0 = jnp.zeros((xp.shape[0], CONV_WIDTH - 1, D_LRU), xp.dtype)
        h00 = jnp.zeros((xp.shape[0], D_LRU), state_lru.dtype)
        xp, cp, hp, _ = mixer_layer(xp, p_prompt[l], buf0, h00, *lw)
        xs, cs, hs, zs = mixer_layer(xs, p_sample[l], state_conv[l], state_lru[l], *lw)
        conv_p.append(cp)
        lru_p.append(hp)
        conv_s.append(cs)
        lru_s.append(hs)
        chunk_s.append(zs)
    conv_prompt = jnp.stack(conv_p)
    lru_prompt = jnp.stack(lru_p)
    conv_sample = jnp.stack(conv_s)
    lru_sample = jnp.stack(lru_s)
    chunk_v_sample = jnp.stack(chunk_s)
    return (xp, xs, conv_prompt, lru_prompt, conv_sample, lru_sample, chunk_v_sample)
```

```python
import contextlib
import numpy as np
import concourse.bass as bass
import concourse.mybir as mybir
from concourse.bass_utils import run_bass_kernel_spmd

F32 = mybir.dt.float32
BF16 = mybir.dt.bfloat16
AF = mybir.ActivationFunctionType
ALU = mybir.AluOpType

NCORES = 8
D = 1024
KC = 8
TPR = 2048
NSQ = 16
TSM = 64
NTOK = TPR + TSM
PLE = 256
EPS = 1e-6
NRING = 8
LOOKAHEAD = 6

PASSES = [
    dict(row0=0, npr=1024, groups=[(0, 512, "p"), (512, 512, "p")],
         subt=[(i * 128, 128) for i in range(8)]),
    dict(row0=1024, npr=1024, groups=[(0, 512, "p"), (512, 512, "p"), (1024, 64, "s")],
         subt=[(i * 128, 128) for i in range(8)] + [(1024, 64)]),
]
NTL = 1088


class Trk:
    def __init__(self, nc, es):
        self.nc = nc
        self.es = es
        self.eng = {"pe": nc.tensor, "act": nc.scalar, "dve": nc.vector, "pool": nc.gpsimd, "sp": nc.sync}
        self.sems = {}
        self.cnt = {}
        self.known = {k: {} for k in self.eng}
        self.bufs = {}
        self.inherit = {}
        self.snaps = {}
        self.first = None
        for k in self.eng:
            self.new_sem("s_" + k)

    def new_sem(self, name):
        self.sems[name] = self.es.enter_context(self.nc.semaphore(name))
        self.cnt[name] = 0

    def note(self, ins):
        if self.first is None:
            self.first = ins
        return ins

    def _learn(self, e, tok):
        kn = self.known[e]
        name, val = tok
        if kn.get(name, 0) < val:
            kn[name] = val
        for k, v in self.snaps.get(tok, {}).items():
            if kn.get(k, 0) < v:
                kn[k] = v

    def wait(self, e, tok):
        if tok is None:
            return
        name, val = tok
        if self.known[e].get(name, 0) >= val:
            return
        self.eng[e].wait_ge(self.sems[name], val)
        self._learn(e, tok)

    def _state(self, k):
        st = self.bufs.get(k)
        if st is None:
            grp = k.split("/")[0]
            st = [None, list(self.inherit.get(grp, []))]
            self.bufs[k] = st
        return st

    def alias(self, newgrp, oldgrps):
        toks = []
        for k, st in self.bufs.items():
            if k.split("/")[0] in oldgrps:
                if st[0] is not None:
                    toks.append(st[0])
                toks.extend(st[1])
        best = {}
        for n, v in toks:
            best[n] = max(best.get(n, 0), v)
        self.inherit[newgrp] = list(best.items())
        for k in [k for k in self.bufs if k.split("/")[0] == newgrp]:
            del self.bufs[k]

    def needed(self, e, reads, writes):
        cand = {}
        def add(tok):
            if tok is not None and cand.get(tok[0], 0) < tok[1]:
                cand[tok[0]] = tok[1]
        for k in reads:
            add(self._state(k)[0])
        for k in writes:
            st = self._state(k)
            add(st[0])
            for r in st[1]:
                add(r)
        toks = sorted(cand.items(), key=lambda t: -len(self.snaps.get(t, {})))
        sim = dict(self.known[e])
        out = []
        for tok in toks:
            if sim.get(tok[0], 0) >= tok[1]:
                continue
            out.append(tok)
            sim[tok[0]] = tok[1]
            for k, v in self.snaps.get(tok, {}).items():
                if sim.get(k, 0) < v:
                    sim[k] = v
        return out

    def commit(self, tok, reads, writes):
        for k in reads:
            self._state(k)[1].append(tok)
        for k in writes:
            self.bufs[k] = [tok, []]

    def op(self, e, fn, reads=(), writes=()):
        need = self.needed(e, reads, writes)
        for tok in need[:-1]:
            self.wait(e, tok)
        self.first = None
        ins = fn()
        first = self.first if self.first is not None else ins
        if need:
            tok = need[-1]
            first._wait_ge(self.sems[tok[0]], tok[1])
            self._learn(e, tok)
        name = "s_" + e
        self.cnt[name] += 1
        ins.then_inc(self.sems[name], 1)
        tok = (name, self.cnt[name])
        snap = dict(self.known[e])
        snap[name] = self.cnt[name]
        self.snaps[tok] = snap
        self.commit(tok, reads, writes)
        return tok

    def dma(self, e, semname, out, in_, reads=(), writes=()):
        need = self.needed(e, reads, writes)
        for tok in need[:-1]:
            self.wait(e, tok)
        if semname not in self.sems:
            self.new_sem(semname)
        ins = self.eng[e].dma_start(out=out, in_=in_)
        if need:
            tok = need[-1]
            ins._wait_ge(self.sems[tok[0]], tok[1])
            self._learn(e, tok)
        self.cnt[semname] += 16
        ins.then_inc(self.sems[semname], 16)
        tok = (semname, self.cnt[semname])
        self.snaps[tok] = dict(self.known[e])
        self.commit(tok, reads, writes)
        return tok


class _Eng:
    def __init__(self, trk, eng):
        self._t = trk
        self._e = eng

    def __getattr__(self, name):
        f = getattr(self._e, name)
        def g(*a, **k):
            return self._t.note(f(*a, **k))
        return g


def build_nc():
    nc = bass.Bass("TRN2", target_bir_lowering=False)

    def din(name, shape):
        return nc.dram_tensor(name, list(shape), F32, kind="ExternalInput").ap()

    def dout(name, shape):
        return nc.dram_tensor(name, list(shape), F32, kind="ExternalOutput").ap()

    xin = din("xin", [NTOK, D])
    pin = din("pin", [NTOK, PLE])
    sconv = din("sconv", [NSQ * 3, D])
    slru = din("slru", [NSQ, D])
    norm_pre = din("norm_pre", [1, D])
    w_in = din("w_in", [D, 5 * D])
    sg_norm = din("sg_norm", [1, D])
    sg_w = din("sg_w", [4, 128, 128])
    sg_b = din("sg_b", [1, 512])
    conv_w = din("conv_w", [4, D])
    conv_b = din("conv_b", [1, D])
    lru_wa = din("lru_wa", [8, 128, 128])
    lru_ba = din("lru_ba", [1, D])
    lru_wx = din("lru_wx", [8, 128, 128])
    lru_bx = din("lru_bx", [1, D])
    lru_lambda = din("lru_lambda", [1, D])
    w_bsg = din("w_branch_sg", [D, D])
    w_blru = din("w_branch_lru", [D, D])
    w_merge = din("w_merge", [D, 2 * D])
    b_merge = din("b_merge", [1, 2 * D])
    w_out = din("w_out", [D, D])
    norm_post = din("norm_post", [1, D])
    w_ple = din("w_ple", [PLE, D])
    w_pg = din("w_ple_gate", [D, D])
    b_pg = din("b_ple_gate", [1, D])

    yout = dout("yout", [NTOK, D])
    finp_o = dout("finp", [4, D])
    fins_o = dout("fins", [64, D])
    chunkv_o = dout("chunkv", [TSM, D])

    with contextlib.ExitStack() as es:
        def sb(name, shape, dt):
            return es.enter_context(nc.sbuf_tensor(name, list(shape), dt))

        T = Trk(nc, es)
        PE = lambda fn, r=(), w=(): T.op("pe", fn, r, w)
        ACT = lambda fn, r=(), w=(): T.op("act", fn, r, w)
        DVE = lambda fn, r=(), w=(): T.op("dve", fn, r, w)
        POOL = lambda fn, r=(), w=(): T.op("pool", fn, r, w)
        VE, SE, TE, GE = _Eng(T, nc.vector), _Eng(T, nc.scalar), _Eng(T, nc.tensor), _Eng(T, nc.gpsimd)
        act = SE.activation
        stt = VE.scalar_tensor_tensor
        tsc = VE.tensor_scalar
        tt = VE.tensor_tensor
        mm = TE.matmul

        hT = sb("hT", [128, KC, NTL], BF16)
        ylru = sb("ylru", [128, KC, NTL], BF16)
        ysg = sb("ysg", [128, KC, NTL], BF16)
        pT = sb("pT", [128, 2, NTL], BF16)
        R1 = sb("R1", [128, 4608], F32)
        R2 = sb("R2", [128, 9472], F32)
        S = sb("S", [128, 7168], F32)
        ring = sb("ring", [128, NRING, KC, 128], BF16)
        wab = sb("wab", [128, 2, 8, 128], BF16)
        bc = sb("bc", [128, 3, D], F32)
        xt = sb("xt", [128, 3, D], F32)
        pb = sb("pb", [128, 2, PLE], BF16)
        xn = sb("xn", [128, 2, D], BF16)
        junk = sb("junk", [128, D], BF16)
        ident_b = sb("ident_b", [128, 128], BF16)
        ident_f = sb("ident_f", [128, 128], F32)
        nhalf = sb("nhalf", [128, 4], F32)
        ones_b = sb("ones_b", [2, 128], BF16)
        pvrow = sb("pvrow", [80, 128], F32)
        pv = sb("pv", [128, 80], F32)
        hb = sb("hb", [128, 32], F32)
        cct = sb("cct", [128, 24], F32)
        WT = sb("WT", [128, 4, 128], BF16)
        WTs = sb("WTs", [64, 4, 64], BF16)
        bhi = sb("bhi", [2, 4, 128], BF16)
        blo = sb("blo", [1, 4, 128], BF16)
        bshi = sb("bshi", [2, 4, 64], BF16)
        bslo = sb("bslo", [1, 4, 64], BF16)
        bghi = sb("bghi", [2, D], BF16)
        bglo = sb("bglo", [1, D], BF16)
        sconvT = sb("sconvT", [128, KC, 48], F32)
        h0T = sb("h0T", [128, KC, 16], F32)
        halo = sb("halo", [128, KC, 3], F32)
        hstate = sb("hstate", [128, KC], F32)
        fin_p = sb("fin_p", [128, KC, 4], F32)
        fin_s = sb("fin_s", [128, KC, 64], F32)
        smalls = sb("smalls", [128, 32], F32)
        xls = sb("xls", [128, 16, 7], F32)
        a_s = sb("a_s", [128, 16, 5], F32)
        w_s = sb("w_s", [128, 16, 5], F32)
        bt_s = sb("bt_s", [128, 16, 5], F32)
        h_s = sb("h_s", [128, 16, 5], F32)
        ix_s = sb("ix_s", [128, 64], F32)
        p_s = sb("p_s", [128, 64], F32)

        psA = es.enter_context(nc.psum_tensor("psA", [128, 6, 512], F32))
        psT = es.enter_context(nc.psum_tensor("psT", [128, 2, KC, 128], BF16))

        st = dict(bank=0, pair=0, tb=0, nbanks=6)

        def next_bank():
            b = st["bank"] % st["nbanks"]
            st["bank"] = (b + 1) % st["nbanks"]
            return b

        def PB(b):
            if b < 6:
                return psA[:, b, :]
            return psT[:, b - 6].rearrange("p k t -> p (k t)").bitcast(F32)

        def pk_(b):
            return f"ps/{b}" if b < 6 else f"psT/{b - 6}"

        def next_pair():
            p = st["pair"]
            st["pair"] = (p + 1) % 3
            return p

        def next_tb():
            t = st["tb"]
            st["tb"] = 1 - t
            return t

        def r2f(off, n):
            return R2[:, off:off + n]

        def bfview(t, off_words, shape):
            nel = int(np.prod(shape))
            v = t[:, off_words:off_words + nel // 2].bitcast(BF16)
            if len(shape) == 2:
                return v.rearrange("p (a b) -> p a b", a=shape[0])
            return v

        setup_keys = []

        def sload(dst, src, key):
            T.dma("sp", "d_setup", dst, src)
            setup_keys.append(key)

        vecs = [conv_w[0:1, :], conv_w[1:2, :], conv_w[2:3, :], conv_w[3:4, :], conv_b, lru_ba, lru_bx, lru_lambda,
                b_merge[:, 0:D], b_merge[:, D:2 * D]]
        for i, v in enumerate(vecs):
            sload(pvrow[i * 8:(i + 1) * 8, :], v.rearrange("o (c p) -> (o c) p", p=128), "pvrow")
        sload(bc[:, 0, :], norm_pre.broadcast_to([128, D]), "bc")
        sload(bc[:, 1, :], sg_norm.broadcast_to([128, D]), "bc")
        sload(bc[:, 2, :], norm_post.broadcast_to([128, D]), "bc")
        wsm = S[:, 5120:5632].rearrange("p (h t) -> p h t", h=4)
        sload(wsm, sg_w.rearrange("h t s -> t h s"), "SD/wsm")
        stc_r = S[0:48, 3072:4096]
        stl_r = S[0:16, 4096:5120]
        sgb_f = S[0:1, 0:512]
        bpg_f = S[0:1, 1024:2048]
        tmp_f = S[0:1, 2048:3072]
        sload(sgb_f, sg_b, "SD/brw")
        sload(bpg_f, b_pg, "SD/brw")
        sload(stc_r, sconv, "SD/stc")
        sload(stl_r, slru, "SD/stl")
        T.dma("pool", "d_wab", wab[:, 0], lru_wa.rearrange("n c d -> c n d"))
        T.dma("pool", "d_wab", wab[:, 1], lru_wx.rearrange("n c d -> c n d"))
        T.bufs["wab"] = [("d_wab", 32), []]
        for k in set(setup_keys):
            T.bufs[k] = [("d_setup", T.cnt["d_setup"]), []]

        POOL(lambda: GE.memset(ident_b[:], 0.0), w=["ident_b"])
        POOL(lambda: GE.affine_select(out=ident_b[:], in_=ident_b[:], compare_op=ALU.not_equal, fill=1.0,
                                             base=0, pattern=[[-1, 128]], channel_multiplier=1), w=["ident_b"])
        POOL(lambda: GE.memset(ident_f[:], 0.0), w=["ident_f"])
        POOL(lambda: GE.affine_select(out=ident_f[:], in_=ident_f[:], compare_op=ALU.not_equal, fill=1.0,
                                             base=0, pattern=[[-1, 128]], channel_multiplier=1), w=["ident_f"])
        POOL(lambda: GE.memset(nhalf[:], -0.5), w=["nhalf"])
        POOL(lambda: GE.memset(ones_b[:], 1.0), w=["ones_b"])
        POOL(lambda: GE.memset(a_s[:], 0.0), w=["a_s"])
        POOL(lambda: GE.memset(WTs[:], 0.0), w=["WTs0"])
        POOL(lambda: GE.memset(halo[:], 0.0), w=["halo"])
        POOL(lambda: GE.memset(hstate[:], 0.0), w=["hstate"])
        POOL(lambda: GE.affine_select(out=wsm, in_=wsm, compare_op=ALU.is_ge, fill=0.0, base=0,
                                             pattern=[[0, 4], [-1, 128]], channel_multiplier=1), w=["SD/wsm"])

        G = {}
        def setup_b():
            b = next_bank()
            PE(lambda: TE.transpose(psA[:, b, 0:80], pvrow[:, :], ident_f[0:80, 0:80]), r=["pvrow", "ident_f"], w=[f"ps/{b}"])
            DVE(lambda: VE.tensor_copy(out=pv[:], in_=psA[:, b, 0:80]), w=[f"ps/{b}", "pv"])
            G['CW'] = lambda j, c: pv[:, j * 8 + c:j * 8 + c + 1]
            G['CB'] = lambda c: pv[:, 32 + c:33 + c]
            DVE(lambda: tsc(out=hb[:, 0:16], in0=pv[:, 40:56], scalar1=0.5, scalar2=None, op0=ALU.mult), r=["pv"], w=["hb"])
            DVE(lambda: tsc(out=hb[:, 16:32], in0=pv[:, 64:80], scalar1=0.5, scalar2=None, op0=ALU.mult), r=["pv"], w=["hb"])
            G['HBA'] = lambda c: hb[:, c:c + 1]
            G['HBX'] = lambda c: hb[:, 8 + c:9 + c]
            G['HBMA'] = lambda c: hb[:, 16 + c:17 + c]
            G['HBMB'] = lambda c: hb[:, 24 + c:25 + c]
            ACT(lambda: act(out=cct[:, 0:8], in_=pv[:, 56:64], func=AF.Exp, scale=-1.0), r=["pv"], w=["cct"])
            ACT(lambda: act(out=cct[:, 0:8], in_=cct[:, 0:8], func=AF.Ln, bias=1.0), w=["cct"])
            DVE(lambda: tsc(out=cct[:, 8:16], in0=cct[:, 0:8], scalar1=-8.0, scalar2=None, op0=ALU.mult), w=["cct"])
            DVE(lambda: tsc(out=cct[:, 16:24], in0=cct[:, 0:8], scalar1=-4.0, scalar2=None, op0=ALU.mult), w=["cct"])
            G['CC'] = lambda c: cct[:, 8 + c:9 + c]
            G['SA'] = lambda c: cct[:, 16 + c:17 + c]

            b = next_bank()
            def _f():
                for h in range(4):
                    i = TE.transpose(psA[:, b, h * 128:(h + 1) * 128], wsm[:, h, :], ident_f[:])
                return i
            PE(_f, r=["SD/wsm", "ident_f"], w=[f"ps/{b}"])
            ACT(lambda: act(out=WT[:].rearrange("p h t -> p (h t)"), in_=psA[:, b, :], func=AF.Copy), w=[f"ps/{b}", "WT"])
            for c in range(KC):
                b = next_bank()
                def _f(c=c, b=b):
                    TE.transpose(psA[:, b, 0:48], stc_r[:, c * 128:(c + 1) * 128], ident_f[0:48, 0:48])
                    return TE.transpose(psA[:, b, 64:80], stl_r[:, c * 128:(c + 1) * 128], ident_f[0:16, 0:16])
                PE(_f, r=["SD/stc", "SD/stl", "ident_f"], w=[f"ps/{b}"])
                def _g(c=c, b=b):
                    VE.tensor_copy(out=sconvT[:, c, :], in_=psA[:, b, 0:48])
                    return VE.tensor_copy(out=h0T[:, c, :], in_=psA[:, b, 64:80])
                DVE(_g, w=[f"ps/{b}", "sconvT", "h0T"])

            def hilo(src, hi, lo, tmp):
                DVE(lambda: VE.tensor_copy(out=hi, in_=src), r=["SD/brw"], w=["bias"])
                DVE(lambda: VE.tensor_copy(out=tmp, in_=hi), w=["bias", "SD/brw2"])
                DVE(lambda: tt(out=lo, in0=src, in1=tmp, op=ALU.subtract), r=["SD/brw"], w=["bias", "SD/brw2"])
            hilo(sgb_f, bhi[0:1].rearrange("o h t -> o (h t)"), blo[:].rearrange("o h t -> o (h t)"), tmp_f[:, 0:512])
            def _f():
                for (dst, src) in ((bshi[0:1], bhi[0:1]), (bslo, blo)):
                    for q in range(16):
                        i = VE.tensor_copy(out=dst[:, :, q * 4:(q + 1) * 4], in_=src[:, :, 0:4])
                return i
            DVE(_f, w=["bias"])
            hilo(bpg_f, bghi[0:1, :], bglo[:], tmp_f)
            T.dma("sp", "d_b2", bhi[1:2], blo[:], reads=["bias"])
            T.dma("sp", "d_b2", bshi[1:2], bslo[:], reads=["bias"])
            T.dma("sp", "d_b2", bghi[1:2, :], bglo[:], reads=["bias"])
            T.bufs["bias2"] = [("d_b2", T.cnt["d_b2"]), []]
            for h in range(4):
                for q in range(16):
                    T.dma("sp", "d_wts", WTs[4 * q:4 * q + 4, h, 4 * q:4 * q + 4], WT[0:4, h, 0:4], reads=["WT", "WTs0"])
            T.bufs["WTs"] = [("d_wts", T.cnt["d_wts"]), []]


        CW = lambda j, c: G["CW"](j, c)
        CB = lambda c: G["CB"](c)
        HBA = lambda c: G["HBA"](c)
        HBX = lambda c: G["HBX"](c)
        HBMA = lambda c: G["HBMA"](c)
        HBMB = lambda c: G["HBMB"](c)
        CC = lambda c: G["CC"](c)
        SA = lambda c: G["SA"](c)
        wq = []
        wstate = dict(issued=0, used=0)

        def chunk_src(w, col0):
            return w[:, col0:col0 + 128].rearrange("(k p) n -> p k n", p=128)

        def ring_issue(upto):
            while wstate["issued"] < min(upto, len(wq)):
                i = wstate["issued"]
                s = i % NRING
                T.dma("pool", f"d_r{s}", ring[:, s], wq[i], writes=[f"ring/{s}"])
                wstate["issued"] += 1

        def ring_get():
            i = wstate["used"]
            wstate["used"] += 1
            ring_issue(i + 1)
            return i % NRING

        def ring_after_use(la=LOOKAHEAD):
            ring_issue(wstate["used"] + la)

        for pi in range(2):
            for c in range(KC):
                wq.append(chunk_src(w_in, D + c * 128))
            for c in range(KC):
                wq.append(chunk_src(w_in, 3 * D + c * 128)); wq.append(chunk_src(w_in, 4 * D + c * 128))
                wq.append(chunk_src(w_in, c * 128)); wq.append(chunk_src(w_in, 2 * D + c * 128))
            for c in range(KC):
                wq.append(chunk_src(w_merge, c * 128)); wq.append(chunk_src(w_merge, D + c * 128))
                wq.append(chunk_src(w_bsg, c * 128)); wq.append(chunk_src(w_blru, c * 128))
        ring_issue(NRING)

        def proj(bank, n, slot, src, l0, extra_r=()):
            def _f():
                for k in range(KC):
                    i = mm(PB(bank)[:, 0:n], lhsT=ring[:, slot, k, :], rhs=src[:, k, l0:l0 + n], start=(k == 0), stop=(k == KC - 1))
                return i
            return _f

        for pi, PS in enumerate(PASSES):
            row0, npr, groups, subt = PS["row0"], PS["npr"], PS["groups"], PS["subt"]
            has_s = len(groups) == 3

            def p0_a(si, l0, n):
                sl = si % 2
                xs = (si + 2) % 3
                c0 = sl * 4
                if not (si == 0 and pi > 0):
                    T.dma("sp", f"d_x{xs}", xt[0:n, xs, :], xin[row0 + l0:row0 + l0 + n, :], writes=[f"xt/{xs}"])
                    T.dma("pool", f"d_p{sl}", pb[0:n, sl, :], pin[row0 + l0:row0 + l0 + n, :], writes=[f"pb/{sl}"])
                ACT(lambda: act(out=junk[0:n, :], in_=xt[0:n, xs, :], func=AF.Square, accum_out=smalls[0:n, c0:c0 + 1]),
                    r=[f"xt/{xs}"], w=["junk", f"sm/{c0}"])
                DVE(lambda: tsc(out=smalls[0:n, c0 + 1:c0 + 2], in0=smalls[0:n, c0:c0 + 1], scalar1=1.0 / D, scalar2=EPS, op0=ALU.mult, op1=ALU.add),
                    r=[f"sm/{c0}"], w=[f"sm/{c0 + 1}"])
                POOL(lambda: GE.tensor_tensor(out=smalls[0:n, c0 + 2:c0 + 3], in0=smalls[0:n, c0 + 1:c0 + 2], in1=nhalf[0:n, 0:1], op=ALU.pow),
                     r=[f"sm/{c0 + 1}", "nhalf"], w=[f"sm/{c0 + 2}"])
                DVE(lambda: stt(out=xn[0:n, sl, :], in0=xt[0:n, xs, :], scalar=smalls[0:n, c0 + 2:c0 + 3], in1=bc[0:n, 0, :], op0=ALU.mult, op1=ALU.mult),
                    r=[f"xt/{xs}", f"sm/{c0 + 2}", "bc"], w=[f"xn/{sl}"])

            def p0_b(si, l0, n):
                sl = si % 2
                tb = next_tb()
                def _f():
                    for k in range(KC):
                        i = TE.transpose(psT[:, tb, k, 0:n], xn[0:n, sl, k * 128:(k + 1) * 128], ident_b[0:n, 0:n])
                    return i
                PE(_f, r=[f"xn/{sl}", "ident_b"], w=[f"psT/{tb}"])
                ACT(lambda: act(out=hT[:, :, l0:l0 + n], in_=psT[:, tb, :, 0:n], func=AF.Copy), w=[f"psT/{tb}", f"hT/{si}"])
                tb2 = next_tb()
                def _f():
                    for j in range(2):
                        i = TE.transpose(psT[:, tb2, j, 0:n], pb[0:n, sl, j * 128:(j + 1) * 128], ident_b[0:n, 0:n])
                    return i
                PE(_f, r=[f"pb/{sl}", "ident_b"], w=[f"psT/{tb2}"])
                ACT(lambda: act(out=pT[:, :, l0:l0 + n], in_=psT[:, tb2, 0:2, 0:n], func=AF.Copy), w=[f"psT/{tb2}", f"pT/{si}"])

            hT_keys = [f"hT/{si}" for si in range(len(subt))]

            def hk(l0, n):
                return [f"hT/{si}" for si, (a, m) in enumerate(subt) if a < l0 + n and a + m > l0]

            T.alias("R1A", ["R1C"])
            T.alias("SA", ["SD"])
            zb = bfview(R1, 0, [9, 1024])
            zf = S[:, 2048:3072]
            for half in range(2):
                slots = [ring_get() for _ in range(4)]
                info = {}

                def av_a(si, l0, n):
                    bk = next_bank()
                    c0 = 20 + (si % 2) * 6
                    info[si] = (bk, c0)
                    assert slots == list(range(slots[0], slots[0] + 4))
                    def _f():
                        for k in range(KC):
                            i = mm(psA[0:n, bk, :].rearrange("p (s t) -> p s t", s=4), lhsT=hT[:, k, l0:l0 + n],
                                   rhs=ring[:, slots[0]:slots[0] + 4, k, :], start=(k == 0), stop=(k == KC - 1))
                        return i
                    PE(_f, r=[f"ring/{s_}" for s_ in slots] + [f"hT/{si}"], w=[f"ps/{bk}"])
                    def _f():
                        for hh in range(2):
                            i = act(out=junk[0:n, hh * 256:(hh + 1) * 256], in_=psA[0:n, bk, hh * 256:(hh + 1) * 256], func=AF.Square,
                                    accum_out=smalls[0:n, c0 + hh:c0 + hh + 1])
                        return i
                    ACT(_f, w=[f"ps/{bk}", "junk", f"sm/{c0}"])
                    DVE(lambda: tsc(out=smalls[0:n, c0 + 2:c0 + 4], in0=smalls[0:n, c0:c0 + 2], scalar1=1.0 / 256, scalar2=EPS, op0=ALU.mult, op1=ALU.add),
                        r=[f"sm/{c0}"], w=[f"sm/{c0 + 2}"])
                    POOL(lambda: GE.tensor_tensor(out=smalls[0:n, c0 + 4:c0 + 6], in0=smalls[0:n, c0 + 2:c0 + 4], in1=nhalf[0:n, 0:2], op=ALU.pow),
                         r=[f"sm/{c0 + 2}", "nhalf"], w=[f"sm/{c0 + 4}"])

                def av_b(si, l0, n):
                    bk, c0 = info[si]
                    is_s = (n == 64)
                    def _f():
                        for hh in range(2):
                            cc0 = half * 512 + hh * 256
                            dst = zf[0:n, cc0:cc0 + 256] if is_s else zb[0:n, si, cc0:cc0 + 256]
                            i = stt(out=dst, in0=psA[0:n, bk, hh * 256:(hh + 1) * 256], scalar=smalls[0:n, c0 + 4 + hh:c0 + 5 + hh],
                                    in1=bc[0:n, 1, cc0:cc0 + 256], op0=ALU.mult, op1=ALU.mult)
                        return i
                    if is_s:
                        DVE(_f, r=[f"sm/{c0 + 4}", "bc"], w=[f"ps/{bk}", f"SA/zf{half}"])
                        DVE(lambda: VE.tensor_copy(out=zb[0:n, si, half * 512:(half + 1) * 512], in_=zf[0:n, half * 512:(half + 1) * 512]),
                            r=[f"SA/zf{half}"], w=[f"R1A/z{si}/{half}"])
                    else:
                        DVE(_f, r=[f"sm/{c0 + 4}", "bc"], w=[f"ps/{bk}", f"R1A/z{si}/{half}"])

                ns_ = len(subt)
                if half == 0:
                    for j in range(ns_ + 3):
                        if j < ns_:
                            p0_a(j, *subt[j])
                        if 1 <= j <= ns_:
                            p0_b(j - 1, *subt[j - 1])
                        if 2 <= j <= ns_ + 1:
                            av_a(j - 2, *subt[j - 2])
                        if 3 <= j <= ns_ + 2:
                            av_b(j - 3, *subt[j - 3])
                    ring_after_use(NRING)
                    if pi == 0:
                        setup_b()
                else:
                    for j in range(ns_ + 1):
                        if j < ns_:
                            av_a(j, *subt[j])
                        if j >= 1:
                            av_b(j - 1, *subt[j - 1])
                    ring_after_use(NRING)
            if has_s:
                T.dma("sp", "d_cv", chunkv_o, zf[0:64, :], reads=["SA/zf0", "SA/zf1"])

            st["nbanks"] = 8
            T.alias("R2B", ["R2D"])
            T.alias("SB", ["SA", "SD"])
            xlbuf = r2f(0, 1028)
            Bb = lambda i, par: r2f(1028 + (i * 2 + par) * 1024, 1024)
            Sv = lambda i: S[:, i * 512:(i + 1) * 512]
            items = [(c, gi) for c in range(KC) for gi in range(len(groups))]
            last_p = max(gi for gi, g in enumerate(groups) if g[2] == "p")
            binfo = {}

            def b_views(idx):
                c, gi = items[idx]
                l0, n, kind = groups[gi]
                par = c % 2
                gp = idx % 2
                d = dict(c=c, gi=gi, l0=l0, n=n, kind=kind, par=par, gp=gp)
                d["xc"] = Sv(gp)[:, 0:n]; d["tr"] = Sv(2 + gp)[:, 0:n]; d["ti"] = Sv(4 + gp)[:, 0:n]; d["tg"] = Sv(6 + gp)[:, 0:n]
                d["xcb"] = bfview(S, 4096 + gp * 256, [512])[:, 0:n]
                d["kx"], d["ktr"], d["kti"], d["ktg"], d["kxb"] = f"SB/xc{gp}", f"SB/tr{gp}", f"SB/ti{gp}", f"SB/tg{gp}", f"SB/xcb{gp}"
                kb = lambda nm: f"R2B/{nm}{par}"
                if kind == "p":
                    d["a"], d["w"], d["ix"], d["p"] = (Bb(i, par)[:, l0:l0 + n] for i in range(4))
                    d["ka"], d["kw"], d["kix"], d["kp"] = (kb(nm) + f"/{gi}" for nm in ("a", "w", "ix", "p"))
                    d["v3"] = lambda ap: ap
                else:
                    d["a"], d["w"], d["ix"], d["p"] = a_s[:, :, 1:5], w_s[:, :, 1:5], ix_s[:, :], p_s[:, :]
                    d["ka"], d["kw"], d["kix"], d["kp"] = "a_s", "w_s", "ix_s", "p_s"
                    d["v3"] = lambda ap: ap.rearrange("p (q t) -> p q t", t=4)
                return d

            def b_s1(idx):
                d = b_views(idx)
                c, gi, l0, n, kind, v3 = d["c"], d["gi"], d["l0"], d["n"], d["kind"], d["v3"]
                xc, tg = d["xc"], d["tg"]
                if gi == 0:
                    binfo[("slots", c)] = (ring_get(), ring_get())
                    DVE(lambda: VE.tensor_copy(out=xlbuf[:, 0:3], in_=halo[:, c, :]), r=["halo"], w=["R2B/xl/h"])
                    if has_s:
                        DVE(lambda: VE.tensor_copy(out=xls[:, :, 0:3], in_=sconvT[:, c, :].rearrange("p (q j) -> p q j", j=3)),
                            r=["sconvT"], w=["xls"])
                sxl, sgl = binfo[("slots", c)]
                b1 = next_bank(); b2 = next_bank()
                d["b1"], d["b2"] = b1, b2
                PE(proj(b1, n, sxl, hT, l0), r=[f"ring/{sxl}"] + hk(l0, n), w=[pk_(b1)])
                PE(proj(b2, n, sgl, hT, l0), r=[f"ring/{sgl}"] + hk(l0, n), w=[pk_(b2)])
                if kind == "p":
                    xlk = f"R2B/xl/{gi}"
                    prevk = [f"R2B/xl/{gi - 1}"] if gi > 0 else ["R2B/xl/h"]
                    ACT(lambda: act(out=xlbuf[:, 3 + l0:3 + l0 + n], in_=PB(b1)[:, 0:n], func=AF.Copy), w=[pk_(b1), xlk])
                    srcs = [xlbuf[:, j + l0:j + l0 + n] for j in range(3)]
                    xcv = xc
                    ckeys = ["pv", xlk] + prevk
                    DVE(lambda: tsc(out=xcv, in0=xlbuf[:, 3 + l0:3 + l0 + n], scalar1=CW(3, c), scalar2=CB(c), op0=ALU.mult, op1=ALU.add),
                        r=ckeys, w=[d["kx"]])
                else:
                    ACT(lambda: act(out=xls[:, :, 3:7], in_=v3(PB(b1)[:, 0:n]), func=AF.Copy), w=[pk_(b1), "xls"])
                    srcs = [xls[:, :, j:j + 4] for j in range(3)]
                    xcv = v3(xc)
                    ckeys = ["pv", "xls"]
                    DVE(lambda: tsc(out=xcv, in0=xls[:, :, 3:7], scalar1=CW(3, c), scalar2=CB(c), op0=ALU.mult, op1=ALU.add),
                        r=ckeys, w=[d["kx"]])
                ACT(lambda: act(out=tg, in_=PB(b2)[:, 0:n], func=AF.Tanh, scale=0.5), w=[pk_(b2), d["ktg"]])
                DVE(lambda: stt(out=d["p"], in0=tg, scalar=1.0, in1=PB(b2)[:, 0:n], op0=ALU.add, op1=ALU.mult), r=[d["ktg"]], w=[pk_(b2), d["kp"]])
                for j in range(3):
                    DVE(lambda: stt(out=xcv, in0=srcs[j], scalar=CW(j, c), in1=xcv, op0=ALU.mult, op1=ALU.add), r=ckeys, w=[d["kx"]])
                DVE(lambda: VE.tensor_copy(out=d["xcb"], in_=xc), r=[d["kx"]], w=[d["kxb"]])
                if kind == "p" and gi == last_p:
                    if pi + 1 < len(PASSES):
                        DVE(lambda: VE.tensor_copy(out=halo[:, c, :], in_=xlbuf[:, npr:npr + 3]), r=[xlk], w=["halo"])
                    else:
                        DVE(lambda: VE.tensor_copy(out=fin_p[:, c, 0:3], in_=xlbuf[:, npr:npr + 3]), r=[xlk], w=["fin_p"])
                if kind == "s":
                    DVE(lambda: VE.tensor_copy(out=fin_s[:, c, 0:48].rearrange("p (q j) -> p q j", j=3), in_=xls[:, :, 4:7]),
                        r=["xls"], w=["fin_s"])
                if gi == len(groups) - 1:
                    ring_after_use()
                binfo[idx] = d

            def b_s2(idx):
                d = binfo.pop(idx)
                c, gi, l0, n, kind, v3, par = d["c"], d["gi"], d["l0"], d["n"], d["kind"], d["v3"], d["par"]
                xc, tr, ti, xcb = d["xc"], d["tr"], d["ti"], d["xcb"]
                b3 = next_bank(); b4 = next_bank()
                PE(lambda: mm(PB(b3)[:, 0:n], lhsT=wab[:, 0, c, :], rhs=xcb, start=True, stop=True), r=["wab", d["kxb"]], w=[pk_(b3)])
                PE(lambda: mm(PB(b4)[:, 0:n], lhsT=wab[:, 1, c, :], rhs=xcb, start=True, stop=True), r=["wab", d["kxb"]], w=[pk_(b4)])
                ACT(lambda: act(out=tr, in_=PB(b3)[:, 0:n], func=AF.Tanh, scale=0.5, bias=HBA(c)), r=["hb"], w=[pk_(b3), d["ktr"]])
                ACT(lambda: act(out=ti, in_=PB(b4)[:, 0:n], func=AF.Tanh, scale=0.5, bias=HBX(c)), r=["hb"], w=[pk_(b4), d["kti"]])
                ACT(lambda: act(out=d["a"], in_=v3(tr), func=AF.Exp, scale=SA(c), bias=SA(c)), r=["cct", d["ktr"]], w=[d["ka"]])
                ACT(lambda: act(out=d["w"], in_=v3(tr), func=AF.Exp, scale=CC(c), bias=CC(c)), r=["cct", d["ktr"]], w=[d["kw"]])
                DVE(lambda: stt(out=d["ix"], in0=ti, scalar=1.0, in1=xc, op0=ALU.add, op1=ALU.mult), r=[d["kti"], d["kx"]], w=[d["kix"]])

            def b_e1(c):
                par = c % 2
                abuf, wbuf, ixbuf, pbuf = Bb(0, par), Bb(1, par), Bb(2, par), Bb(3, par)
                kb = lambda nm: f"R2B/{nm}{par}"
                gkeys = lambda nm: [kb(nm) + f"/{g_}" for g_, g in enumerate(groups) if g[2] == "p"]
                ACT(lambda: act(out=wbuf[:, 0:npr], in_=wbuf[:, 0:npr], func=AF.Sqrt, scale=-0.25, bias=0.25), w=gkeys("w"))
                if has_s:
                    ACT(lambda: act(out=w_s[:, :, 1:5], in_=w_s[:, :, 1:5], func=AF.Sqrt, scale=-0.25, bias=0.25), w=["w_s"])
                POOL(lambda: GE.tensor_tensor(out=ixbuf[:, 0:npr], in0=wbuf[:, 0:npr], in1=ixbuf[:, 0:npr], op=ALU.mult), r=gkeys("w"), w=gkeys("ix"))

            def b_e2(c):
                par = c % 2
                abuf, wbuf, ixbuf, pbuf = Bb(0, par), Bb(1, par), Bb(2, par), Bb(3, par)
                kb = lambda nm: f"R2B/{nm}{par}"
                gkeys = lambda nm: [kb(nm) + f"/{g_}" for g_, g in enumerate(groups) if g[2] == "p"]
                DVE(lambda: VE.tensor_tensor_scan(out=wbuf[:, 0:npr], data0=abuf[:, 0:npr], data1=ixbuf[:, 0:npr],
                                                         initial=hstate[:, c:c + 1], op0=ALU.mult, op1=ALU.add),
                    r=gkeys("a") + gkeys("ix") + ["hstate"], w=gkeys("w"))
                DVE(lambda: tt(out=ylru[:, c, 0:npr], in0=wbuf[:, 0:npr], in1=pbuf[:, 0:npr], op=ALU.mult),
                    r=gkeys("w") + gkeys("p"), w=[f"ylru/{c}"])
                if pi + 1 < len(PASSES):
                    DVE(lambda: VE.tensor_copy(out=hstate[:, c:c + 1], in_=wbuf[:, npr - 1:npr]), r=gkeys("w"), w=["hstate"])
                else:
                    DVE(lambda: VE.tensor_copy(out=fin_p[:, c, 3:4], in_=wbuf[:, npr - 1:npr]), r=gkeys("w"), w=["fin_p"])
                if has_s:
                    def _f():
                        VE.tensor_copy(out=bt_s[:, :, 0], in_=h0T[:, c, :])
                        return tt(out=bt_s[:, :, 1:5], in0=w_s[:, :, 1:5], in1=ix_s[:, :].rearrange("p (q t) -> p q t", t=4), op=ALU.mult)
                    DVE(_f, r=["h0T", "w_s", "ix_s"], w=["bt_s"])
                    DVE(lambda: VE.tensor_tensor_scan(out=h_s[:].rearrange("p q t -> p (q t)"), data0=a_s[:].rearrange("p q t -> p (q t)"),
                                                             data1=bt_s[:].rearrange("p q t -> p (q t)"), initial=0.0, op0=ALU.mult, op1=ALU.add),
                        r=["a_s", "bt_s"], w=["h_s"])
                    def _f():
                        tt(out=ylru[:, c, 1024:1088].rearrange("p (q t) -> p q t", t=4), in0=h_s[:, :, 1:5],
                           in1=p_s[:, :].rearrange("p (q t) -> p q t", t=4), op=ALU.mult)
                        return VE.tensor_copy(out=fin_s[:, c, 48:64], in_=h_s[:, :, 4])
                    DVE(_f, r=["h_s", "p_s"], w=[f"ylru/s{c}", "fin_s"])

            zb = bfview(R1, 0, [9, 1024])
            ainfo = {}

            def a_item(idx):
                c, gi = items[idx]
                l0, n, kind = groups[gi]
                h = c // 2
                gp = idx % 2
                if gi == 0:
                    ainfo[c] = (ring_get(), ring_get())
                su, sg_ = ainfo[c]
                bu = next_bank(); bg = next_bank(); bs = next_bank()
                PE(proj(bu, n, su, hT, l0), r=[f"ring/{su}"] + hk(l0, n), w=[pk_(bu)])
                PE(proj(bg, n, sg_, hT, l0), r=[f"ring/{sg_}"] + hk(l0, n), w=[pk_(bg)])
                def _f():
                    if kind == "p":
                        for j in range(n // 128):
                            mm(PB(bs)[:, j * 128:(j + 1) * 128], lhsT=ones_b[0:2, :], rhs=bhi[0:2, h, :], start=(j == 0), stop=False, skip_group_check=True)
                        for j in range(n // 128):
                            sj = (l0 + j * 128) // 128
                            i = mm(PB(bs)[:, j * 128:(j + 1) * 128], lhsT=zb[:, sj, c * 128:(c + 1) * 128], rhs=WT[:, h, :],
                                   start=False, stop=True, skip_group_check=True)
                    else:
                        mm(PB(bs)[:, 0:n], lhsT=ones_b[0:2, :], rhs=bshi[0:2, h, :], start=True, stop=False)
                        i = mm(PB(bs)[:, 0:n], lhsT=zb[0:64, 8, c * 128:(c + 1) * 128], rhs=WTs[:, h, :], start=False, stop=True)
                    return i
                zkeys = [f"R1A/z{si}/{c // 4}" for si, (a, m) in enumerate(subt) if a < l0 + n and a + m > l0]
                PE(_f, r=["ones_b", "bias", "bias2", "WT"] + (["WTs"] if kind == "s" else []) + zkeys, w=[pk_(bs)])
                sgt = S[:, 4608 + gp * 512:4608 + gp * 512 + n]
                pu = S[:, 5632 + gp * 512:5632 + gp * 512 + n]
                ACT(lambda: act(out=sgt, in_=PB(bg)[:, 0:n], func=AF.Tanh, scale=0.5), w=[pk_(bg), f"SB/sgt{gp}"])
                DVE(lambda: stt(out=pu, in0=sgt, scalar=1.0, in1=PB(bg)[:, 0:n], op0=ALU.add, op1=ALU.mult), r=[f"SB/sgt{gp}"],
                    w=[pk_(bg), f"SB/pu{gp}"])
                DVE(lambda: tt(out=pu, in0=pu, in1=PB(bu)[:, 0:n], op=ALU.mult), w=[pk_(bu), f"SB/pu{gp}"])
                DVE(lambda: tt(out=ysg[:, c, l0:l0 + n], in0=pu, in1=PB(bs)[:, 0:n], op=ALU.mult), r=[f"SB/pu{gp}"],
                    w=[pk_(bs), f"ysg/{c}/{gi}"])
                if gi == len(groups) - 1:
                    ring_after_use()

            for j in range(len(items) + 1):
                if j < len(items):
                    b_s1(j)
                if j >= 2:
                    cj2, gj2 = items[j - 2]
                    if gj2 == len(groups) - 1:
                        b_e1(cj2)
                if j >= 1:
                    a_item(j - 1)
                    b_s2(j - 1)
                    cj, gj = items[j - 1]
                    if gj == 0 and cj >= 1:
                        b_e2(cj - 1)
            b_e1(KC - 1)
            b_e2(KC - 1)
            ring_after_use(NRING)
            ylk = lambda c, kind: [f"ylru/{c}"] if kind == "p" else [f"ylru/s{c}"]
            T.alias("R2D", ["R2B"])
            wo = bfview(R2, 0, [KC, D])
            wgt = bfview(R2, 4096, [KC, D])
            wpl = bfview(R2, 8192, [2, D])

            dw_pieces = [("d_wo", wo[:, k, :], w_out[k * 128:(k + 1) * 128, :], f"R2D/wo/{k}") for k in range(KC)]
            dw_pieces += [("d_wg", wgt[:, k, :], w_pg[k * 128:(k + 1) * 128, :], f"R2D/wg/{k}") for k in range(KC)]
            dw_pieces += [("d_wp", wpl, w_ple.rearrange("(k p) n -> p k n", p=128), "R2D/wp")]
            dw_state = dict(i=0)

            def load_d_piece(n_=1):
                for _ in range(n_):
                    if dw_state["i"] < len(dw_pieces):
                        sem_, dst_, src_, key_ = dw_pieces[dw_state["i"]]
                        T.dma("pool", sem_, dst_, src_, writes=[key_])
                        dw_state["i"] += 1
            WO_KEYS = [f"R2D/wo/{k}" for k in range(KC)]
            WG_KEYS = [f"R2D/wg/{k}" for k in range(KC)]

            T.alias("R1C", ["R1A"])
            T.alias("SC", ["SB"])
            mg = bfview(R1, 0, [KC, NTL])
            for c in range(KC):
                sma = ring_get(); smb = ring_get(); ssg = ring_get(); slr = ring_get()
                cinfo = {}

                def c_gates(gi):
                    l0, n, kind = groups[gi]
                    gp = gi % 2
                    ba_ = next_bank(); bb_ = next_bank()
                    cinfo[gi] = (ba_, bb_)
                    PE(proj(ba_, n, sma, hT, l0), r=[f"ring/{sma}"] + hk(l0, n), w=[pk_(ba_)])
                    PE(proj(bb_, n, smb, hT, l0), r=[f"ring/{smb}"] + hk(l0, n), w=[pk_(bb_)])
                    ta = S[:, gp * 512:gp * 512 + n]
                    tb_ = S[:, 1024 + gp * 512:1024 + gp * 512 + n]
                    ACT(lambda: act(out=ta, in_=PB(ba_)[:, 0:n], func=AF.Tanh, scale=0.5, bias=HBMA(c)), r=["hb"], w=[pk_(ba_), f"SC/ta{gp}"])
                    ACT(lambda: act(out=tb_, in_=PB(bb_)[:, 0:n], func=AF.Tanh, scale=0.5, bias=HBMB(c)), r=["hb"], w=[pk_(bb_), f"SC/tb{gp}"])

                def c_branches(gi):
                    l0, n, kind = groups[gi]
                    gp = gi % 2
                    bA = next_bank(); bB = next_bank()
                    PE(proj(bA, n, ssg, ysg, l0), r=[f"ring/{ssg}"] + [f"ysg/{k}/{gi}" for k in range(KC)], w=[pk_(bA)])
                    PE(proj(bB, n, slr, ylru, l0), r=[f"ring/{slr}"] + sum([ylk(k, kind) for k in range(KC)], []), w=[pk_(bB)])
                    ta = S[:, gp * 512:gp * 512 + n]
                    tb_ = S[:, 1024 + gp * 512:1024 + gp * 512 + n]
                    t1 = S[:, 2048 + gp * 512:2048 + gp * 512 + n]
                    t2 = S[:, 3072 + gp * 512:3072 + gp * 512 + n]
                    DVE(lambda: stt(out=t1, in0=ta, scalar=1.0, in1=PB(bA)[:, 0:n], op0=ALU.add, op1=ALU.mult), r=[f"SC/ta{gp}"],
                        w=[pk_(bA), f"SC/t1{gp}"])
                    DVE(lambda: stt(out=t2, in0=tb_, scalar=1.0, in1=PB(bB)[:, 0:n], op0=ALU.add, op1=ALU.mult), r=[f"SC/tb{gp}"],
                        w=[pk_(bB), f"SC/t2{gp}"])
                    DVE(lambda: tt(out=mg[:, c, l0:l0 + n], in0=t2, in1=t1, op=ALU.add),
                        r=[f"SC/t1{gp}", f"SC/t2{gp}"], w=[f"R1C/mg/{c}/{gi}"])

                if c == 0:
                    c_gates(0); c_gates(1); c_branches(0); c_branches(1)
                    for gi in range(2, len(groups)):
                        c_gates(gi); c_branches(gi)
                else:
                    for gi in range(len(groups)):
                        c_gates(gi); c_branches(gi)
                ring_after_use(NRING)
                if c >= 2:
                    load_d_piece(3)

            load_d_piece(len(dw_pieces))
            st["nbanks"] = 6
            st["bank"] = 0
            T.alias("SD", ["SC"])
            x1v = lambda par: S[:, par * 1024:(par + 1) * 1024]
            x1bv = lambda par: bfview(S, 2048 + par * 512, [1024])
            x1Tv = lambda par: bfview(S, 3072 + par * 512, [KC, 128])
            dinfo = {}

            def d_xload(si):
                l0, n = subt[si]
                sl = si % 2
                T.dma("sp", f"d_x{sl}", xt[0:n, sl, :], xin[row0 + l0:row0 + l0 + n, :], writes=[f"xt/{sl}"])

            def d_a1_pe(si, l0, n):
                par = si % 2
                pr = par
                gi = [i for i, g in enumerate(groups) if g[0] <= l0 < g[0] + g[1]][0]
                def _f():
                    for hf in range(2):
                        for k in range(KC):
                            i = mm(psA[0:n, 2 * pr + hf, :], lhsT=mg[:, k, l0:l0 + n], rhs=wo[:, k, hf * 512:(hf + 1) * 512],
                                   start=(k == 0), stop=(k == KC - 1))
                    return i
                pk = [f"ps/{2 * pr}", f"ps/{2 * pr + 1}"]
                PE(_f, r=WO_KEYS + [f"R1C/mg/{k}/{gi}" for k in range(KC)], w=pk)

            def d_a1_rest(si, l0, n):
                par = si % 2
                pr = par
                c0 = 12 + par * 4
                pk = [f"ps/{2 * pr}", f"ps/{2 * pr + 1}"]
                o2d = psA[0:n, 2 * pr:2 * pr + 2, :].rearrange("p a b -> p (a b)")
                ACT(lambda: act(out=junk[0:n, :], in_=o2d, func=AF.Square, accum_out=smalls[0:n, c0:c0 + 1]), w=pk + ["junk", f"sm/{c0}"])
                DVE(lambda: tsc(out=smalls[0:n, c0 + 1:c0 + 2], in0=smalls[0:n, c0:c0 + 1], scalar1=1.0 / D, scalar2=16.0 * EPS, op0=ALU.mult, op1=ALU.add),
                    r=[f"sm/{c0}"], w=[f"sm/{c0 + 1}"])
                POOL(lambda: GE.tensor_tensor(out=smalls[0:n, c0 + 2:c0 + 3], in0=smalls[0:n, c0 + 1:c0 + 2], in1=nhalf[0:n, 0:1], op=ALU.pow),
                     r=[f"sm/{c0 + 1}", "nhalf"], w=[f"sm/{c0 + 2}"])

            def d_a2(si, l0, n):
                sl = si % 2
                par = si % 2
                pr = par
                c0 = 12 + par * 4
                pk = [f"ps/{2 * pr}", f"ps/{2 * pr + 1}"]
                o2d = psA[0:n, 2 * pr:2 * pr + 2, :].rearrange("p a b -> p (a b)")
                x1 = x1v(par)
                DVE(lambda: stt(out=x1[0:n, :], in0=o2d, scalar=smalls[0:n, c0 + 2:c0 + 3], in1=bc[0:n, 2, :], op0=ALU.mult, op1=ALU.mult),
                    r=[f"sm/{c0 + 2}", "bc"], w=pk + [f"SD/x1{par}"])
                DVE(lambda: tt(out=x1[0:n, :], in0=x1[0:n, :], in1=xt[0:n, sl, :], op=ALU.add), r=[f"xt/{sl}"], w=[f"SD/x1{par}"])
                DVE(lambda: VE.tensor_copy(out=x1bv(par)[0:n, :], in_=x1[0:n, :]), r=[f"SD/x1{par}"], w=[f"SD/x1b{par}"])

            def d_a2_act(si, l0, n):
                par = si % 2
                if si + 2 < len(subt):
                    d_xload(si + 2)

            def d_b1(si, l0, n):
                par = si % 2
                x1, x1b, x1T = x1v(par), x1bv(par), x1Tv(par)
                tb = 0
                pe_ps = psT[:, 1].rearrange("p k t -> p (k t)").bitcast(F32)
                def _f():
                    for k in range(KC):
                        i = TE.transpose(psT[:, tb, k, 0:n], x1b[0:n, k * 128:(k + 1) * 128], ident_b[0:n, 0:n])
                    return i
                PE(_f, r=[f"SD/x1b{par}", "ident_b"], w=[f"psT/{tb}"])
                ACT(lambda: act(out=x1T[:, :, 0:n], in_=psT[:, tb, :, 0:n], func=AF.Copy), w=[f"psT/{tb}", f"SD/x1T{par}"])

            def d_b2(si, l0, n):
                par = si % 2
                x1, x1b, x1T = x1v(par), x1bv(par), x1Tv(par)
                pe_ps = psT[:, 1].rearrange("p k t -> p (k t)").bitcast(F32)
                yv = S[:, 5120 + par * 1024:6144 + par * 1024]
                for hf in range(2):
                    bgt = 4 + hf
                    cs = slice(hf * 512, (hf + 1) * 512)
                    def _f():
                        mm(psA[0:n, bgt, :], lhsT=ones_b[0:2, 0:n], rhs=bghi[0:2, cs], start=True, stop=False)
                        for k in range(KC):
                            i = mm(psA[0:n, bgt, :], lhsT=x1T[:, k, 0:n], rhs=wgt[:, k, cs], start=False, stop=(k == KC - 1))
                        return i
                    PE(_f, r=[f"SD/x1T{par}", "ones_b", "bias", "bias2"] + WG_KEYS, w=[f"ps/{bgt}"])
                    def _f():
                        for j in range(2):
                            i = mm(pe_ps[0:n, :], lhsT=pT[:, j, l0:l0 + n], rhs=wpl[:, j, cs], start=(j == 0), stop=(j == 1))
                        return i
                    PE(_f, r=[f"pT/{si}", "R2D/wp"], w=["psT/1"])
                    tgt = S[:, 4096 + hf * 512:4608 + hf * 512]
                    ACT(lambda: act(out=tgt[0:n, :], in_=psA[0:n, bgt, :], func=AF.Tanh, scale=0.5), w=[f"ps/{bgt}", f"SD/tgt{hf}"])
                    DVE(lambda: stt(out=tgt[0:n, :], in0=tgt[0:n, :], scalar=1.0, in1=pe_ps[0:n, :], op0=ALU.add, op1=ALU.mult),
                        w=["psT/1", f"SD/tgt{hf}"])
                    DVE(lambda: stt(out=yv[0:n, cs], in0=tgt[0:n, :], scalar=0.5, in1=x1[0:n, cs], op0=ALU.mult, op1=ALU.add),
                        r=[f"SD/tgt{hf}", f"SD/x1{par}"], w=[f"SD/y{par}/{hf}"])
                T.dma("sp", f"d_y{par}", yout[row0 + l0:row0 + l0 + n, :], yv[0:n, :], reads=[f"SD/y{par}/0", f"SD/y{par}/1"])

            d_xload(0)
            d_xload(1)
            if pi + 1 < len(PASSES):
                nrow0 = PASSES[pi + 1]["row0"]
                nl0, nn = PASSES[pi + 1]["subt"][0]
                T.dma("sp", "d_x2", xt[0:nn, 2, :], xin[nrow0 + nl0:nrow0 + nl0 + nn, :], writes=["xt/2"])
                T.dma("pool", "d_p0", pb[0:nn, 0, :], pin[nrow0 + nl0:nrow0 + nl0 + nn, :], writes=["pb/0"])
            ns_ = len(subt)
            d_a1_pe(0, *subt[0])
            d_a1_rest(0, *subt[0])
            def d_warm(nd):
                def _f():
                    for _ in range(nd):
                        i = mm(psA[:, 5, :], lhsT=ident_b[:], rhs=wo[:, 0, 0:512], start=True, stop=True)
                    return i
                PE(_f, r=["ident_b", "R2D/wo/0"], w=["ps/5"])

            d_a2(0, *subt[0])
            d_a2_act(0, *subt[0])
            d_a1_pe(1, *subt[1])
            for j in range(ns_):
                d_b1(j, *subt[j])
                if j + 1 < ns_:
                    if j + 1 != 1:
                        d_a1_pe(j + 1, *subt[j + 1])
                    d_a1_rest(j + 1, *subt[j + 1])
                    d_a2(j + 1, *subt[j + 1])
                else:
                    d_warm(6)
                d_b2(j, *subt[j])
                if j + 1 < ns_:
                    d_a2_act(j + 1, *subt[j + 1])

        pr = next_pair()
        def _f():
            for c in range(KC):
                i = TE.transpose(psA[0:4, 2 * pr + c // 4, (c % 4) * 128:(c % 4 + 1) * 128], fin_p[:, c, :], ident_f[:])
            return i
        pk = [f"ps/{2 * pr}", f"ps/{2 * pr + 1}"]
        PE(_f, r=["fin_p", "ident_f"], w=pk)
        fin_pr = xt[0:4, 0, :]
        fin_sr = xt[0:64, 1, :]
        DVE(lambda: VE.tensor_copy(out=fin_pr, in_=psA[0:4, 2 * pr:2 * pr + 2, :].rearrange("p a b -> p (a b)")), w=pk + ["xt/0"])
        T.dma("sp", "d_fp", finp_o, fin_pr, reads=["xt/0"])
        pr = next_pair()
        def _f():
            for c in range(KC):
                i = TE.transpose(psA[0:64, 2 * pr + c // 4, (c % 4) * 128:(c % 4 + 1) * 128], fin_s[:, c, :], ident_f[:])
            return i
        pk = [f"ps/{2 * pr}", f"ps/{2 * pr + 1}"]
        PE(_f, r=["fin_s", "ident_f"], w=pk)
        DVE(lambda: VE.tensor_copy(out=fin_sr, in_=psA[0:64, 2 * pr:2 * pr + 2, :].rearrange("p a b -> p (a b)")), w=pk + ["xt/1"])
        T.dma("sp", "d_fs", fins_o, fin_sr, reads=["xt/1"])

        for name in ("d_y0", "d_y1", "d_cv", "d_fp", "d_fs"):
            T.wait("sp", (name, T.cnt[name]))
    return nc


def kernel(x_prompt, x_sample, p_prompt, p_sample, state_conv, state_lru,
           norm_pre, w_in, sg_norm, sg_w, sg_b, conv_w, conv_b,
           lru_wa, lru_ba, lru_wx, lru_bx, lru_lambda, w_branch_sg, w_branch_lru,
           w_merge, b_merge, w_out, norm_post, w_ple, w_ple_gate, b_ple_gate):
    f = lambda a: np.ascontiguousarray(np.asarray(a, dtype=np.float32))
    x_prompt, x_sample, p_prompt, p_sample = f(x_prompt), f(x_sample), f(p_prompt), f(p_sample)
    state_conv, state_lru = f(state_conv), f(state_lru)
    shared = {
        "norm_pre": f(norm_pre).reshape(1, D), "w_in": f(w_in)[0], "sg_norm": f(sg_norm).reshape(1, D),
        "sg_w": f(sg_w)[0], "sg_b": f(sg_b).reshape(1, 512), "conv_w": f(conv_w)[0], "conv_b": f(conv_b).reshape(1, D),
        "lru_wa": f(lru_wa)[0], "lru_ba": f(lru_ba).reshape(1, D), "lru_wx": f(lru_wx)[0], "lru_bx": f(lru_bx).reshape(1, D),
        "lru_lambda": f(lru_lambda).reshape(1, D), "w_branch_sg": f(w_branch_sg)[0], "w_branch_lru": f(w_branch_lru)[0],
        "w_merge": f(w_merge)[0], "b_merge": f(b_merge).reshape(1, 2 * D), "w_out": f(w_out)[0],
        "norm_post": f(norm_post).reshape(1, D), "w_ple": f(w_ple)[0], "w_ple_gate": f(w_ple_gate)[0],
        "b_ple_gate": f(b_ple_gate).reshape(1, D),
    }
    in_maps = []
    for i in range(NCORES):
        sq = slice(NSQ * i, NSQ * (i + 1))
        m = dict(shared)
        m["xin"] = np.concatenate([x_prompt[i], x_sample[sq].reshape(TSM, D)], axis=0)
        m["pin"] = np.concatenate([p_prompt[0, i], p_sample[0, sq].reshape(TSM, PLE)], axis=0)
        m["sconv"] = state_conv[0, sq].reshape(NSQ * 3, D)
        m["slru"] = state_lru[0, sq]
        in_maps.append(m)
    nc = build_nc()
    res = run_bass_kernel_spmd(nc, in_maps, core_ids=list(range(NCORES)))
    R = res.results
    y_prompt = np.stack([R[i]["yout"][:TPR] for i in range(NCORES)], 0)
    y_sample = np.concatenate([R[i]["yout"][TPR:].reshape(NSQ, 4, D) for i in range(NCORES)], 0)
    conv_prompt = np.stack([R[i]["finp"][0:3] for i in range(NCORES)], 0)[None]
    lru_prompt = np.stack([R[i]["finp"][3] for i in range(NCORES)], 0)[None]
    conv_sample = np.concatenate([R[i]["fins"][0:48].reshape(NSQ, 3, D) for i in range(NCORES)], 0)[None]
    lru_sample = np.concatenate([R[i]["fins"][48:64] for i in range(NCORES)], 0)[None]
    chunk_v = np.concatenate([R[i]["chunkv"].reshape(NSQ, 4, D) for i in range(NCORES)], 0)[None]
    o = lambda a: np.ascontiguousarray(a, dtype=np.float32)
    return (o(y_prompt), o(y_sample), o(conv_prompt), o(lru_prompt), o(conv_sample), o(lru_sample), o(chunk_v))
```

```python
import contextlib
import numpy as np
import concourse.bass as bass
import concourse.mybir as mybir
from concourse.bass_utils import run_bass_kernel_spmd

F32 = mybir.dt.float32
BF16 = mybir.dt.bfloat16
AF = mybir.ActivationFunctionType
ALU = mybir.AluOpType

NCORES = 8
D = 1024
KC = 8
TPR = 2048
NSQ = 16
TSM = 64
NTOK = TPR + TSM
PLE = 256
EPS = 1e-6
NRING = 8
LOOKAHEAD = 6

PASSES = [
    dict(row0=0, npr=1024, groups=[(0, 512, "p"), (512, 512, "p")],
         subt=[(i * 128, 128) for i in range(8)]),
    dict(row0=1024, npr=1024, groups=[(0, 512, "p"), (512, 512, "p"), (1024, 64, "s")],
         subt=[(i * 128, 128) for i in range(8)] + [(1024, 64)]),
]
NTL = 1088


class Trk:
    def __init__(self, nc, es):
        self.nc = nc
        self.es = es
        self.eng = {"pe": nc.tensor, "act": nc.scalar, "dve": nc.vector, "pool": nc.gpsimd, "sp": nc.sync}
        self.sems = {}
        self.cnt = {}
        self.known = {k: {} for k in self.eng}
        self.bufs = {}
        self.inherit = {}
        self.snaps = {}
        self.first = None
        for k in self.eng:
            self.new_sem("s_" + k)

    def new_sem(self, name):
        self.sems[name] = self.es.enter_context(self.nc.semaphore(name))
        self.cnt[name] = 0

    def note(self, ins):
        if self.first is None:
            self.first = ins
        return ins

    def _learn(self, e, tok):
        kn = self.known[e]
        name, val = tok
        if kn.get(name, 0) < val:
            kn[name] = val
        for k, v in self.snaps.get(tok, {}).items():
            if kn.get(k, 0) < v:
                kn[k] = v

    def wait(self, e, tok):
        if tok is None:
            return
        name, val = tok
        if self.known[e].get(name, 0) >= val:
            return
        self.eng[e].wait_ge(self.sems[name], val)
        self._learn(e, tok)

    def _state(self, k):
        st = self.bufs.get(k)
        if st is None:
            grp = k.split("/")[0]
            st = [None, list(self.inherit.get(grp, []))]
            self.bufs[k] = st
        return st

    def alias(self, newgrp, oldgrps):
        toks = []
        for k, st in self.bufs.items():
            if k.split("/")[0] in oldgrps:
                if st[0] is not None:
                    toks.append(st[0])
                toks.extend(st[1])
        best = {}
        for n, v in toks:
            best[n] = max(best.get(n, 0), v)
        self.inherit[newgrp] = list(best.items())
        for k in [k for k in self.bufs if k.split("/")[0] == newgrp]:
            del self.bufs[k]

    def needed(self, e, reads, writes):
        cand = {}
        def add(tok):
            if tok is not None and cand.get(tok[0], 0) < tok[1]:
                cand[tok[0]] = tok[1]
        for k in reads:
            add(self._state(k)[0])
        for k in writes:
            st = self._state(k)
            add(st[0])
            for r in st[1]:
                add(r)
        toks = sorted(cand.items(), key=lambda t: -len(self.snaps.get(t, {})))
        sim = dict(self.known[e])
        out = []
        for tok in toks:
            if sim.get(tok[0], 0) >= tok[1]:
                continue
            out.append(tok)
            sim[tok[0]] = tok[1]
            for k, v in self.snaps.get(tok, {}).items():
                if sim.get(k, 0) < v:
                    sim[k] = v
        return out

    def commit(self, tok, reads, writes):
        for k in reads:
            self._state(k)[1].append(tok)
        for k in writes:
            self.bufs[k] = [tok, []]

    def op(self, e, fn, reads=(), writes=()):
        need = self.needed(e, reads, writes)
        for tok in need[:-1]:
            self.wait(e, tok)
        self.first = None
        ins = fn()
        first = self.first if self.first is not None else ins
        if need:
            tok = need[-1]
            first._wait_ge(self.sems[tok[0]], tok[1])
            self._learn(e, tok)
        name = "s_" + e
        self.cnt[name] += 1
        ins.then_inc(self.sems[name], 1)
        tok = (name, self.cnt[name])
        snap = dict(self.known[e])
        snap[name] = self.cnt[name]
        self.snaps[tok] = snap
        self.commit(tok, reads, writes)
        return tok

    def dma(self, e, semname, out, in_, reads=(), writes=()):
        need = self.needed(e, reads, writes)
        for tok in need[:-1]:
            self.wait(e, tok)
        if semname not in self.sems:
            self.new_sem(semname)
        ins = self.eng[e].dma_start(out=out, in_=in_)
        if need:
            tok = need[-1]
            ins._wait_ge(self.sems[tok[0]], tok[1])
            self._learn(e, tok)
        self.cnt[semname] += 16
        ins.then_inc(self.sems[semname], 16)
        tok = (semname, self.cnt[semname])
        self.snaps[tok] = dict(self.known[e])
        self.commit(tok, reads, writes)
        return tok


class _Eng:
    def __init__(self, trk, eng):
        self._t = trk
        self._e = eng

    def __getattr__(self, name):
        f = getattr(self._e, name)
        def g(*a, **k):
            return self._t.note(f(*a, **k))
        return g


def build_nc():
    nc = bass.Bass("TRN2", target_bir_lowering=False)

    def din(name, shape):
        return nc.dram_tensor(name, list(shape), F32, kind="ExternalInput").ap()

    def dout(name, shape):
        return nc.dram_tensor(name, list(shape), F32, kind="ExternalOutput").ap()

    xin = din("xin", [NTOK, D])
    pin = din("pin", [NTOK, PLE])
    sconv = din("sconv", [NSQ * 3, D])
    slru = din("slru", [NSQ, D])
    norm_pre = din("norm_pre", [1, D])
    w_in = din("w_in", [D, 5 * D])
    sg_norm = din("sg_norm", [1, D])
    sg_w = din("sg_w", [4, 128, 128])
    sg_b = din("sg_b", [1, 512])
    conv_w = din("conv_w", [4, D])
    conv_b = din("conv_b", [1, D])
    lru_wa = din("lru_wa", [8, 128, 128])
    lru_ba = din("lru_ba", [1, D])
    lru_wx = din("lru_wx", [8, 128, 128])
    lru_bx = din("lru_bx", [1, D])
    lru_lambda = din("lru_lambda", [1, D])
    w_bsg = din("w_branch_sg", [D, D])
    w_blru = din("w_branch_lru", [D, D])
    w_merge = din("w_merge", [D, 2 * D])
    b_merge = din("b_merge", [1, 2 * D])
    w_out = din("w_out", [D, D])
    norm_post = din("norm_post", [1, D])
    w_ple = din("w_ple", [PLE, D])
    w_pg = din("w_ple_gate", [D, D])
    b_pg = din("b_ple_gate", [1, D])

    yout = dout("yout", [NTOK, D])
    finp_o = dout("finp", [4, D])
    fins_o = dout("fins", [64, D])
    chunkv_o = dout("chunkv", [TSM, D])

    with contextlib.ExitStack() as es:
        def sb(name, shape, dt):
            return es.enter_context(nc.sbuf_tensor(name, list(shape), dt))

        T = Trk(nc, es)
        PE = lambda fn, r=(), w=(): T.op("pe", fn, r, w)
        ACT = lambda fn, r=(), w=(): T.op("act", fn, r, w)
        DVE = lambda fn, r=(), w=(): T.op("dve", fn, r, w)
        POOL = lambda fn, r=(), w=(): T.op("pool", fn, r, w)
        VE, SE, TE, GE = _Eng(T, nc.vector), _Eng(T, nc.scalar), _Eng(T, nc.tensor), _Eng(T, nc.gpsimd)
        act = SE.activation
        stt = VE.scalar_tensor_tensor
        tsc = VE.tensor_scalar
        tt = VE.tensor_tensor
        mm = TE.matmul

        hT = sb("hT", [128, KC, NTL], BF16)
        ylru = sb("ylru", [128, KC, NTL], BF16)
        ysg = sb("ysg", [128, KC, NTL], BF16)
        pT = sb("pT", [128, 2, NTL], BF16)
        R1 = sb("R1", [128, 4608], F32)
        R2 = sb("R2", [128, 9472], F32)
        S = sb("S", [128, 7168], F32)
        ring = sb("ring", [128, NRING, KC, 128], BF16)
        wab = sb("wab", [128, 2, 8, 128], BF16)
        bc = sb("bc", [128, 3, D], F32)
        xt = sb("xt", [128, 3, D], F32)
        pb = sb("pb", [128, 2, PLE], BF16)
        xn = sb("xn", [128, 2, D], BF16)
        junk = sb("junk", [128, D], BF16)
        ident_b = sb("ident_b", [128, 128], BF16)
        ident_f = sb("ident_f", [128, 128], F32)
        nhalf = sb("nhalf", [128, 4], F32)
        ones_b = sb("ones_b", [2, 128], BF16)
        pvrow = sb("pvrow", [80, 128], F32)
        pv = sb("pv", [128, 80], F32)
        hb = sb("hb", [128, 32], F32)
        cct = sb("cct", [128, 24], F32)
        WT = sb("WT", [128, 4, 128], BF16)
        WTs = sb("WTs", [64, 4, 64], BF16)
        bhi = sb("bhi", [2, 4, 128], BF16)
        blo = sb("blo", [1, 4, 128], BF16)
        bshi = sb("bshi", [2, 4, 64], BF16)
        bslo = sb("bslo", [1, 4, 64], BF16)
        bghi = sb("bghi", [2, D], BF16)
        bglo = sb("bglo", [1, D], BF16)
        sconvT = sb("sconvT", [128, KC, 48], F32)
        h0T = sb("h0T", [128, KC, 16], F32)
        halo = sb("halo", [128, KC, 3], F32)
        hstate = sb("hstate", [128, KC], F32)
        fin_p = sb("fin_p", [128, KC, 4], F32)
        fin_s = sb("fin_s", [128, KC, 64], F32)
        smalls = sb("smalls", [128, 32], F32)
        xls = sb("xls", [128, 16, 7], F32)
        a_s = sb("a_s", [128, 16, 5], F32)
        w_s = sb("w_s", [128, 16, 5], F32)
        bt_s = sb("bt_s", [128, 16, 5], F32)
        h_s = sb("h_s", [128, 16, 5], F32)
        ix_s = sb("ix_s", [128, 64], F32)
        p_s = sb("p_s", [128, 64], F32)

        psA = es.enter_context(nc.psum_tensor("psA", [128, 6, 512], F32))
        psT = es.enter_context(nc.psum_tensor("psT", [128, 2, KC, 128], BF16))

        st = dict(bank=0, pair=0, tb=0, nbanks=6)

        def next_bank():
            b = st["bank"] % st["nbanks"]
            st["bank"] = (b + 1) % st["nbanks"]
            return b

        def PB(b):
            if b < 6:
                return psA[:, b, :]
            return psT[:, b - 6].rearrange("p k t -> p (k t)").bitcast(F32)

        def pk_(b):
            return f"ps/{b}" if b < 6 else f"psT/{b - 6}"

        def next_pair():
            p = st["pair"]
            st["pair"] = (p + 1) % 3
            return p

        def next_tb():
            t = st["tb"]
            st["tb"] = 1 - t
            return t

        def r2f(off, n):
            return R2[:, off:off + n]

        def bfview(t, off_words, shape):
            nel = int(np.prod(shape))
            v = t[:, off_words:off_words + nel // 2].bitcast(BF16)
            if len(shape) == 2:
                return v.rearrange("p (a b) -> p a b", a=shape[0])
            return v

        setup_keys = []

        def sload(dst, src, key):
            T.dma("sp", "d_setup", dst, src)
            setup_keys.append(key)

        vecs = [conv_w[0:1, :], conv_w[1:2, :], conv_w[2:3, :], conv_w[3:4, :], conv_b, lru_ba, lru_bx, lru_lambda,
                b_merge[:, 0:D], b_merge[:, D:2 * D]]
        for i, v in enumerate(vecs):
            sload(pvrow[i * 8:(i + 1) * 8, :], v.rearrange("o (c p) -> (o c) p", p=128), "pvrow")
        sload(bc[:, 0, :], norm_pre.broadcast_to([128, D]), "bc")
        sload(bc[:, 1, :], sg_norm.broadcast_to([128, D]), "bc")
        sload(bc[:, 2, :], norm_post.broadcast_to([128, D]), "bc")
        wsm = S[:, 5120:5632].rearrange("p (h t) -> p h t", h=4)
        sload(wsm, sg_w.rearrange("h t s -> t h s"), "SD/wsm")
        stc_r = S[0:48, 3072:4096]
        stl_r = S[0:16, 4096:5120]
        sgb_f = S[0:1, 0:512]
        bpg_f = S[0:1, 1024:2048]
        tmp_f = S[0:1, 2048:3072]
        sload(sgb_f, sg_b, "SD/brw")
        sload(bpg_f, b_pg, "SD/brw")
        sload(stc_r, sconv, "SD/stc")
        sload(stl_r, slru, "SD/stl")
        T.dma("pool", "d_wab", wab[:, 0], lru_wa.rearrange("n c d -> c n d"))
        T.dma("pool", "d_wab", wab[:, 1], lru_wx.rearrange("n c d -> c n d"))
        T.bufs["wab"] = [("d_wab", 32), []]
        for k in set(setup_keys):
            T.bufs[k] = [("d_setup", T.cnt["d_setup"]), []]

        POOL(lambda: GE.memset(ident_b[:], 0.0), w=["ident_b"])
        POOL(lambda: GE.affine_select(out=ident_b[:], in_=ident_b[:], compare_op=ALU.not_equal, fill=1.0,
                                             base=0, pattern=[[-1, 128]], channel_multiplier=1), w=["ident_b"])
        POOL(lambda: GE.memset(ident_f[:], 0.0), w=["ident_f"])
        POOL(lambda: GE.affine_select(out=ident_f[:], in_=ident_f[:], compare_op=ALU.not_equal, fill=1.0,
                                             base=0, pattern=[[-1, 128]], channel_multiplier=1), w=["ident_f"])
        POOL(lambda: GE.memset(nhalf[:], -0.5), w=["nhalf"])
        POOL(lambda: GE.memset(ones_b[:], 1.0), w=["ones_b"])
        POOL(lambda: GE.memset(a_s[:], 0.0), w=["a_s"])
        POOL(lambda: GE.memset(WTs[:], 0.0), w=["WTs0"])
        POOL(lambda: GE.memset(halo[:], 0.0), w=["halo"])
        POOL(lambda: GE.memset(hstate[:], 0.0), w=["hstate"])
        POOL(lambda: GE.affine_select(out=wsm, in_=wsm, compare_op=ALU.is_ge, fill=0.0, base=0,
                                             pattern=[[0, 4], [-1, 128]], channel_multiplier=1), w=["SD/wsm"])

        G = {}
        def setup_b():
            b = next_bank()
            PE(lambda: TE.transpose(psA[:, b, 0:80], pvrow[:, :], ident_f[0:80, 0:80]), r=["pvrow", "ident_f"], w=[f"ps/{b}"])
            DVE(lambda: VE.tensor_copy(out=pv[:], in_=psA[:, b, 0:80]), w=[f"ps/{b}", "pv"])
            G['CW'] = lambda j, c: pv[:, j * 8 + c:j * 8 + c + 1]
            G['CB'] = lambda c: pv[:, 32 + c:33 + c]
            DVE(lambda: tsc(out=hb[:, 0:16], in0=pv[:, 40:56], scalar1=0.5, scalar2=None, op0=ALU.mult), r=["pv"], w=["hb"])
            DVE(lambda: tsc(out=hb[:, 16:32], in0=pv[:, 64:80], scalar1=0.5, scalar2=None, op0=ALU.mult), r=["pv"], w=["hb"])
            G['HBA'] = lambda c: hb[:, c:c + 1]
            G['HBX'] = lambda c: hb[:, 8 + c:9 + c]
            G['HBMA'] = lambda c: hb[:, 16 + c:17 + c]
            G['HBMB'] = lambda c: hb[:, 24 + c:25 + c]
            ACT(lambda: act(out=cct[:, 0:8], in_=pv[:, 56:64], func=AF.Exp, scale=-1.0), r=["pv"], w=["cct"])
            ACT(lambda: act(out=cct[:, 0:8], in_=cct[:, 0:8], func=AF.Ln, bias=1.0), w=["cct"])
            DVE(lambda: tsc(out=cct[:, 8:16], in0=cct[:, 0:8], scalar1=-8.0, scalar2=None, op0=ALU.mult), w=["cct"])
            DVE(lambda: tsc(out=cct[:, 16:24], in0=cct[:, 0:8], scalar1=-4.0, scalar2=None, op0=ALU.mult), w=["cct"])
            G['CC'] = lambda c: cct[:, 8 + c:9 + c]
            G['SA'] = lambda c: cct[:, 16 + c:17 + c]

            b = next_bank()
            def _f():
                for h in range(4):
                    i = TE.transpose(psA[:, b, h * 128:(h + 1) * 128], wsm[:, h, :], ident_f[:])
                return i
            PE(_f, r=["SD/wsm", "ident_f"], w=[f"ps/{b}"])
            ACT(lambda: act(out=WT[:].rearrange("p h t -> p (h t)"), in_=psA[:, b, :], func=AF.Copy), w=[f"ps/{b}", "WT"])
            for c in range(KC):
                b = next_bank()
                def _f(c=c, b=b):
                    TE.transpose(psA[:, b, 0:48], stc_r[:, c * 128:(c + 1) * 128], ident_f[0:48, 0:48])
                    return TE.transpose(psA[:, b, 64:80], stl_r[:, c * 128:(c + 1) * 128], ident_f[0:16, 0:16])
                PE(_f, r=["SD/stc", "SD/stl", "ident_f"], w=[f"ps/{b}"])
                def _g(c=c, b=b):
                    VE.tensor_copy(out=sconvT[:, c, :], in_=psA[:, b, 0:48])
                    return VE.tensor_copy(out=h0T[:, c, :], in_=psA[:, b, 64:80])
                DVE(_g, w=[f"ps/{b}", "sconvT", "h0T"])

            def hilo(src, hi, lo, tmp):
                DVE(lambda: VE.tensor_copy(out=hi, in_=src), r=["SD/brw"], w=["bias"])
                DVE(lambda: VE.tensor_copy(out=tmp, in_=hi), w=["bias", "SD/brw2"])
                DVE(lambda: tt(out=lo, in0=src, in1=tmp, op=ALU.subtract), r=["SD/brw"], w=["bias", "SD/brw2"])
            hilo(sgb_f, bhi[0:1].rearrange("o h t -> o (h t)"), blo[:].rearrange("o h t -> o (h t)"), tmp_f[:, 0:512])
            def _f():
                for (dst, src) in ((bshi[0:1], bhi[0:1]), (bslo, blo)):
                    for q in range(16):
                        i = VE.tensor_copy(out=dst[:, :, q * 4:(q + 1) * 4], in_=src[:, :, 0:4])
                return i
            DVE(_f, w=["bias"])
            hilo(bpg_f, bghi[0:1, :], bglo[:], tmp_f)
            T.dma("sp", "d_b2", bhi[1:2], blo[:], reads=["bias"])
            T.dma("sp", "d_b2", bshi[1:2], bslo[:], reads=["bias"])
            T.dma("sp", "d_b2", bghi[1:2, :], bglo[:], reads=["bias"])
            T.bufs["bias2"] = [("d_b2", T.cnt["d_b2"]), []]
            for h in range(4):
                for q in range(16):
                    T.dma("sp", "d_wts", WTs[4 * q:4 * q + 4, h, 4 * q:4 * q + 4], WT[0:4, h, 0:4], reads=["WT", "WTs0"])
            T.bufs["WTs"] = [("d_wts", T.cnt["d_wts"]), []]


        CW = lambda j, c: G["CW"](j, c)
        CB = lambda c: G["CB"](c)
        HBA = lambda c: G["HBA"](c)
        HBX = lambda c: G["HBX"](c)
        HBMA = lambda c: G["HBMA"](c)
        HBMB = lambda c: G["HBMB"](c)
        CC = lambda c: G["CC"](c)
        SA = lambda c: G["SA"](c)
        wq = []
        wstate = dict(issued=0, used=0)

        def chunk_src(w, col0):
            return w[:, col0:col0 + 128].rearrange("(k p) n -> p k n", p=128)

        def ring_issue(upto):
            while wstate["issued"] < min(upto, len(wq)):
                i = wstate["issued"]
                s = i % NRING
                T.dma("pool", f"d_r{s}", ring[:, s], wq[i], writes=[f"ring/{s}"])
                wstate["issued"] += 1

        def ring_get():
            i = wstate["used"]
            wstate["used"] += 1
            ring_issue(i + 1)
            return i % NRING

        def ring_after_use(la=LOOKAHEAD):
            ring_issue(wstate["used"] + la)

        for pi in range(2):
            for c in range(KC):
                wq.append(chunk_src(w_in, D + c * 128))
            for c in range(KC):
                wq.append(chunk_src(w_in, 3 * D + c * 128)); wq.append(chunk_src(w_in, 4 * D + c * 128))
                wq.append(chunk_src(w_in, c * 128)); wq.append(chunk_src(w_in, 2 * D + c * 128))
            for c in range(KC):
                wq.append(chunk_src(w_merge, c * 128)); wq.append(chunk_src(w_merge, D + c * 128))
                wq.append(chunk_src(w_bsg, c * 128)); wq.append(chunk_src(w_blru, c * 128))
        ring_issue(NRING)

        def proj(bank, n, slot, src, l0, extra_r=()):
            def _f():
                for k in range(KC):
                    i = mm(PB(bank)[:, 0:n], lhsT=ring[:, slot, k, :], rhs=src[:, k, l0:l0 + n], start=(k == 0), stop=(k == KC - 1))
                return i
            return _f

        for pi, PS in enumerate(PASSES):
            row0, npr, groups, subt = PS["row0"], PS["npr"], PS["groups"], PS["subt"]
            has_s = len(groups) == 3

            def p0_a(si, l0, n):
                sl = si % 2
                xs = (si + 2) % 3
                c0 = sl * 4
                if not (si == 0 and pi > 0):
                    T.dma("sp", f"d_x{xs}", xt[0:n, xs, :], xin[row0 + l0:row0 + l0 + n, :], writes=[f"xt/{xs}"])
                    T.dma("pool", f"d_p{sl}", pb[0:n, sl, :], pin[row0 + l0:row0 + l0 + n, :], writes=[f"pb/{sl}"])
                ACT(lambda: act(out=junk[0:n, :], in_=xt[0:n, xs, :], func=AF.Square, accum_out=smalls[0:n, c0:c0 + 1]),
                    r=[f"xt/{xs}"], w=["junk", f"sm/{c0}"])
                DVE(lambda: tsc(out=smalls[0:n, c0 + 1:c0 + 2], in0=smalls[0:n, c0:c0 + 1], scalar1=1.0 / D, scalar2=EPS, op0=ALU.mult, op1=ALU.add),
                    r=[f"sm/{c0}"], w=[f"sm/{c0 + 1}"])
                POOL(lambda: GE.tensor_tensor(out=smalls[0:n, c0 + 2:c0 + 3], in0=smalls[0:n, c0 + 1:c0 + 2], in1=nhalf[0:n, 0:1], op=ALU.pow),
                     r=[f"sm/{c0 + 1}", "nhalf"], w=[f"sm/{c0 + 2}"])
                DVE(lambda: stt(out=xn[0:n, sl, :], in0=xt[0:n, xs, :], scalar=smalls[0:n, c0 + 2:c0 + 3], in1=bc[0:n, 0, :], op0=ALU.mult, op1=ALU.mult),
                    r=[f"xt/{xs}", f"sm/{c0 + 2}", "bc"], w=[f"xn/{sl}"])

            def p0_b(si, l0, n):
                sl = si % 2
                tb = next_tb()
                def _f():
                    for k in range(KC):
                        i = TE.transpose(psT[:, tb, k, 0:n], xn[0:n, sl, k * 128:(k + 1) * 128], ident_b[0:n, 0:n])
                    return i
                PE(_f, r=[f"xn/{sl}", "ident_b"], w=[f"psT/{tb}"])
                ACT(lambda: act(out=hT[:, :, l0:l0 + n], in_=psT[:, tb, :, 0:n], func=AF.Copy), w=[f"psT/{tb}", f"hT/{si}"])
                tb2 = next_tb()
                def _f():
                    for j in range(2):
                        i = TE.transpose(psT[:, tb2, j, 0:n], pb[0:n, sl, j * 128:(j + 1) * 128], ident_b[0:n, 0:n])
                    return i
                PE(_f, r=[f"pb/{sl}", "ident_b"], w=[f"psT/{tb2}"])
                ACT(lambda: act(out=pT[:, :, l0:l0 + n], in_=psT[:, tb2, 0:2, 0:n], func=AF.Copy), w=[f"psT/{tb2}", f"pT/{si}"])

            hT_keys = [f"hT/{si}" for si in range(len(subt))]

            def hk(l0, n):
                return [f"hT/{si}" for si, (a, m) in enumerate(subt) if a < l0 + n and a + m > l0]

            T.alias("R1A", ["R1C"])
            T.alias("SA", ["SD"])
            zb = bfview(R1, 0, [9, 1024])
            zf = S[:, 2048:3072]
            for half in range(2):
                slots = [ring_get() for _ in range(4)]
                info = {}

                def av_a(si, l0, n):
                    bk = next_bank()
                    c0 = 20 + (si % 2) * 6
                    info[si] = (bk, c0)
                    assert slots == list(range(slots[0], slots[0] + 4))
                    def _f():
                        for k in range(KC):
                            i = mm(psA[0:n, bk, :].rearrange("p (s t) -> p s t", s=4), lhsT=hT[:, k, l0:l0 + n],
                                   rhs=ring[:, slots[0]:slots[0] + 4, k, :], start=(k == 0), stop=(k == KC - 1))
                        return i
                    PE(_f, r=[f"ring/{s_}" for s_ in slots] + [f"hT/{si}"], w=[f"ps/{bk}"])
                    def _f():
                        for hh in range(2):
                            i = act(out=junk[0:n, hh * 256:(hh + 1) * 256], in_=psA[0:n, bk, hh * 256:(hh + 1) * 256], func=AF.Square,
                                    accum_out=smalls[0:n, c0 + hh:c0 + hh + 1])
                        return i
                    ACT(_f, w=[f"ps/{bk}", "junk", f"sm/{c0}"])
                    DVE(lambda: tsc(out=smalls[0:n, c0 + 2:c0 + 4], in0=smalls[0:n, c0:c0 + 2], scalar1=1.0 / 256, scalar2=EPS, op0=ALU.mult, op1=ALU.add),
                        r=[f"sm/{c0}"], w=[f"sm/{c0 + 2}"])
                    POOL(lambda: GE.tensor_tensor(out=smalls[0:n, c0 + 4:c0 + 6], in0=smalls[0:n, c0 + 2:c0 + 4], in1=nhalf[0:n, 0:2], op=ALU.pow),
                         r=[f"sm/{c0 + 2}", "nhalf"], w=[f"sm/{c0 + 4}"])

                def av_b(si, l0, n):
                    bk, c0 = info[si]
                    is_s = (n == 64)
                    def _f():
                        for hh in range(2):
                            cc0 = half * 512 + hh * 256
                            dst = zf[0:n, cc0:cc0 + 256] if is_s else zb[0:n, si, cc0:cc0 + 256]
                            i = stt(out=dst, in0=psA[0:n, bk, hh * 256:(hh + 1) * 256], scalar=smalls[0:n, c0 + 4 + hh:c0 + 5 + hh],
                                    in1=bc[0:n, 1, cc0:cc0 + 256], op0=ALU.mult, op1=ALU.mult)
                        return i
                    if is_s:
                        DVE(_f, r=[f"sm/{c0 + 4}", "bc"], w=[f"ps/{bk}", f"SA/zf{half}"])
                        DVE(lambda: VE.tensor_copy(out=zb[0:n, si, half * 512:(half + 1) * 512], in_=zf[0:n, half * 512:(half + 1) * 512]),
                            r=[f"SA/zf{half}"], w=[f"R1A/z{si}/{half}"])
                    else:
                        DVE(_f, r=[f"sm/{c0 + 4}", "bc"], w=[f"ps/{bk}", f"R1A/z{si}/{half}"])

                ns_ = len(subt)
                if half == 0:
                    for j in range(ns_ + 3):
                        if j < ns_:
                            p0_a(j, *subt[j])
                        if 1 <= j <= ns_:
                            p0_b(j - 1, *subt[j - 1])
                        if 2 <= j <= ns_ + 1:
                            av_a(j - 2, *subt[j - 2])
                        if 3 <= j <= ns_ + 2:
                            av_b(j - 3, *subt[j - 3])
                    ring_after_use(NRING)
                    if pi == 0:
                        setup_b()
                else:
                    for j in range(ns_ + 1):
                        if j < ns_:
                            av_a(j, *subt[j])
                        if j >= 1:
                            av_b(j - 1, *subt[j - 1])
                    ring_after_use(NRING)
            if has_s:
                T.dma("sp", "d_cv", chunkv_o, zf[0:64, :], reads=["SA/zf0", "SA/zf1"])

            st["nbanks"] = 8
            T.alias("R2B", ["R2D"])
            T.alias("SB", ["SA", "SD"])
            xlbuf = r2f(0, 1028)
            Bb = lambda i, par: r2f(1028 + (i * 2 + par) * 1024, 1024)
            Sv = lambda i: S[:, i * 512:(i + 1) * 512]
            items = [(c, gi) for c in range(KC) for gi in range(len(groups))]
            last_p = max(gi for gi, g in enumerate(groups) if g[2] == "p")
            binfo = {}

            def b_views(idx):
                c, gi = items[idx]
                l0, n, kind = groups[gi]
                par = c % 2
                gp = idx % 2
                d = dict(c=c, gi=gi, l0=l0, n=n, kind=kind, par=par, gp=gp)
                d["xc"] = Sv(gp)[:, 0:n]; d["tr"] = Sv(2 + gp)[:, 0:n]; d["ti"] = Sv(4 + gp)[:, 0:n]; d["tg"] = Sv(6 + gp)[:, 0:n]
                d["xcb"] = bfview(S, 4096 + gp * 256, [512])[:, 0:n]
                d["kx"], d["ktr"], d["kti"], d["ktg"], d["kxb"] = f"SB/xc{gp}", f"SB/tr{gp}", f"SB/ti{gp}", f"SB/tg{gp}", f"SB/xcb{gp}"
                kb = lambda nm: f"R2B/{nm}{par}"
                if kind == "p":
                    d["a"], d["w"], d["ix"], d["p"] = (Bb(i, par)[:, l0:l0 + n] for i in range(4))
                    d["ka"], d["kw"], d["kix"], d["kp"] = (kb(nm) + f"/{gi}" for nm in ("a", "w", "ix", "p"))
                    d["v3"] = lambda ap: ap
                else:
                    d["a"], d["w"], d["ix"], d["p"] = a_s[:, :, 1:5], w_s[:, :, 1:5], ix_s[:, :], p_s[:, :]
                    d["ka"], d["kw"], d["kix"], d["kp"] = "a_s", "w_s", "ix_s", "p_s"
                    d["v3"] = lambda ap: ap.rearrange("p (q t) -> p q t", t=4)
                return d

            def b_s1(idx):
                d = b_views(idx)
                c, gi, l0, n, kind, v3 = d["c"], d["gi"], d["l0"], d["n"], d["kind"], d["v3"]
                xc, tg = d["xc"], d["tg"]
                if gi == 0:
                    binfo[("slots", c)] = (ring_get(), ring_get())
                    if pi > 0 or c == 0:
                        DVE(lambda: VE.tensor_copy(out=xlbuf[:, 0:3], in_=halo[:, c, :]), r=["halo"], w=["R2B/xl/h"])
                    if has_s:
                        DVE(lambda: VE.tensor_copy(out=xls[:, :, 0:3], in_=sconvT[:, c, :].rearrange("p (q j) -> p q j", j=3)),
                            r=["sconvT"], w=["xls"])
                sxl, sgl = binfo[("slots", c)]
                b1 = next_bank(); b2 = next_bank()
                d["b1"], d["b2"] = b1, b2
                PE(proj(b1, n, sxl, hT, l0), r=[f"ring/{sxl}"] + hk(l0, n), w=[pk_(b1)])
                PE(proj(b2, n, sgl, hT, l0), r=[f"ring/{sgl}"] + hk(l0, n), w=[pk_(b2)])
                if kind == "p":
                    xlk = f"R2B/xl/{gi}"
                    prevk = [f"R2B/xl/{gi - 1}"] if gi > 0 else ["R2B/xl/h"]
                    ACT(lambda: act(out=xlbuf[:, 3 + l0:3 + l0 + n], in_=PB(b1)[:, 0:n], func=AF.Copy), w=[pk_(b1), xlk])
                    srcs = [xlbuf[:, j + l0:j + l0 + n] for j in range(3)]
                    xcv = xc
                    ckeys = ["pv", xlk] + prevk
                    DVE(lambda: tsc(out=xcv, in0=xlbuf[:, 3 + l0:3 + l0 + n], scalar1=CW(3, c), scalar2=CB(c), op0=ALU.mult, op1=ALU.add),
                        r=ckeys, w=[d["kx"]])
                else:
                    ACT(lambda: act(out=xls[:, :, 3:7], in_=v3(PB(b1)[:, 0:n]), func=AF.Copy), w=[pk_(b1), "xls"])
                    srcs = [xls[:, :, j:j + 4] for j in range(3)]
                    xcv = v3(xc)
                    ckeys = ["pv", "xls"]
                    DVE(lambda: tsc(out=xcv, in0=xls[:, :, 3:7], scalar1=CW(3, c), scalar2=CB(c), op0=ALU.mult, op1=ALU.add),
                        r=ckeys, w=[d["kx"]])
                ACT(lambda: act(out=tg, in_=PB(b2)[:, 0:n], func=AF.Tanh, scale=0.5), w=[pk_(b2), d["ktg"]])
                DVE(lambda: stt(out=d["p"], in0=tg, scalar=1.0, in1=PB(b2)[:, 0:n], op0=ALU.add, op1=ALU.mult), r=[d["ktg"]], w=[pk_(b2), d["kp"]])
                for j in range(3):
                    DVE(lambda: stt(out=xcv, in0=srcs[j], scalar=CW(j, c), in1=xcv, op0=ALU.mult, op1=ALU.add), r=ckeys, w=[d["kx"]])
                DVE(lambda: VE.tensor_copy(out=d["xcb"], in_=xc), r=[d["kx"]], w=[d["kxb"]])
                if kind == "p" and gi == last_p:
                    if pi + 1 < len(PASSES):
                        DVE(lambda: VE.tensor_copy(out=halo[:, c, :], in_=xlbuf[:, npr:npr + 3]), r=[xlk], w=["halo"])
                    else:
                        DVE(lambda: VE.tensor_copy(out=fin_p[:, c, 0:3], in_=xlbuf[:, npr:npr + 3]), r=[xlk], w=["fin_p"])
                if kind == "s":
                    DVE(lambda: VE.tensor_copy(out=fin_s[:, c, 0:48].rearrange("p (q j) -> p q j", j=3), in_=xls[:, :, 4:7]),
                        r=["xls"], w=["fin_s"])
                if gi == len(groups) - 1:
                    ring_after_use()
                binfo[idx] = d

            def b_s2(idx):
                d = binfo.pop(idx)
                c, gi, l0, n, kind, v3, par = d["c"], d["gi"], d["l0"], d["n"], d["kind"], d["v3"], d["par"]
                xc, tr, ti, xcb = d["xc"], d["tr"], d["ti"], d["xcb"]
                b3 = next_bank(); b4 = next_bank()
                PE(lambda: mm(PB(b3)[:, 0:n], lhsT=wab[:, 0, c, :], rhs=xcb, start=True, stop=True), r=["wab", d["kxb"]], w=[pk_(b3)])
                PE(lambda: mm(PB(b4)[:, 0:n], lhsT=wab[:, 1, c, :], rhs=xcb, start=True, stop=True), r=["wab", d["kxb"]], w=[pk_(b4)])
                ACT(lambda: act(out=tr, in_=PB(b3)[:, 0:n], func=AF.Tanh, scale=0.5, bias=HBA(c)), r=["hb"], w=[pk_(b3), d["ktr"]])
                ACT(lambda: act(out=ti, in_=PB(b4)[:, 0:n], func=AF.Tanh, scale=0.5, bias=HBX(c)), r=["hb"], w=[pk_(b4), d["kti"]])
                ACT(lambda: act(out=d["a"], in_=v3(tr), func=AF.Exp, scale=SA(c), bias=SA(c)), r=["cct", d["ktr"]], w=[d["ka"]])
                ACT(lambda: act(out=d["w"], in_=v3(tr), func=AF.Exp, scale=CC(c), bias=CC(c)), r=["cct", d["ktr"]], w=[d["kw"]])
                DVE(lambda: stt(out=d["ix"], in0=ti, scalar=1.0, in1=xc, op0=ALU.add, op1=ALU.mult), r=[d["kti"], d["kx"]], w=[d["kix"]])

            def b_e1(c):
                par = c % 2
                abuf, wbuf, ixbuf, pbuf = Bb(0, par), Bb(1, par), Bb(2, par), Bb(3, par)
                kb = lambda nm: f"R2B/{nm}{par}"
                gkeys = lambda nm: [kb(nm) + f"/{g_}" for g_, g in enumerate(groups) if g[2] == "p"]
                ACT(lambda: act(out=wbuf[:, 0:npr], in_=wbuf[:, 0:npr], func=AF.Sqrt, scale=-0.25, bias=0.25), w=gkeys("w"))
                if has_s:
                    ACT(lambda: act(out=w_s[:, :, 1:5], in_=w_s[:, :, 1:5], func=AF.Sqrt, scale=-0.25, bias=0.25), w=["w_s"])
                POOL(lambda: GE.tensor_tensor(out=ixbuf[:, 0:npr], in0=wbuf[:, 0:npr], in1=ixbuf[:, 0:npr], op=ALU.mult), r=gkeys("w"), w=gkeys("ix"))

            def b_e2(c):
                par = c % 2
                abuf, wbuf, ixbuf, pbuf = Bb(0, par), Bb(1, par), Bb(2, par), Bb(3, par)
                kb = lambda nm: f"R2B/{nm}{par}"
                gkeys = lambda nm: [kb(nm) + f"/{g_}" for g_, g in enumerate(groups) if g[2] == "p"]
                DVE(lambda: VE.tensor_tensor_scan(out=wbuf[:, 0:npr], data0=abuf[:, 0:npr], data1=ixbuf[:, 0:npr],
                                                         initial=hstate[:, c:c + 1], op0=ALU.mult, op1=ALU.add),
                    r=gkeys("a") + gkeys("ix") + ["hstate"], w=gkeys("w"))
                DVE(lambda: tt(out=ylru[:, c, 0:npr], in0=wbuf[:, 0:npr], in1=pbuf[:, 0:npr], op=ALU.mult),
                    r=gkeys("w") + gkeys("p"), w=[f"ylru/{c}"])
                if pi + 1 < len(PASSES):
                    DVE(lambda: VE.tensor_copy(out=hstate[:, c:c + 1], in_=wbuf[:, npr - 1:npr]), r=gkeys("w"), w=["hstate"])
                else:
                    DVE(lambda: VE.tensor_copy(out=fin_p[:, c, 3:4], in_=wbuf[:, npr - 1:npr]), r=gkeys("w"), w=["fin_p"])
                if has_s:
                    def _f():
                        VE.tensor_copy(out=bt_s[:, :, 0], in_=h0T[:, c, :])
                        return tt(out=bt_s[:, :, 1:5], in0=w_s[:, :, 1:5], in1=ix_s[:, :].rearrange("p (q t) -> p q t", t=4), op=ALU.mult)
                    DVE(_f, r=["h0T", "w_s", "ix_s"], w=["bt_s"])
                    DVE(lambda: VE.tensor_tensor_scan(out=h_s[:].rearrange("p q t -> p (q t)"), data0=a_s[:].rearrange("p q t -> p (q t)"),
                                                             data1=bt_s[:].rearrange("p q t -> p (q t)"), initial=0.0, op0=ALU.mult, op1=ALU.add),
                        r=["a_s", "bt_s"], w=["h_s"])
                    def _f():
                        tt(out=ylru[:, c, 1024:1088].rearrange("p (q t) -> p q t", t=4), in0=h_s[:, :, 1:5],
                           in1=p_s[:, :].rearrange("p (q t) -> p q t", t=4), op=ALU.mult)
                        return VE.tensor_copy(out=fin_s[:, c, 48:64], in_=h_s[:, :, 4])
                    DVE(_f, r=["h_s", "p_s"], w=[f"ylru/s{c}", "fin_s"])

            zb = bfview(R1, 0, [9, 1024])
            ainfo = {}

            def a_item(idx):
                c, gi = items[idx]
                l0, n, kind = groups[gi]
                h = c // 2
                gp = idx % 2
                if gi == 0:
                    ainfo[c] = (ring_get(), ring_get())
                su, sg_ = ainfo[c]
                bu = next_bank(); bg = next_bank(); bs = next_bank()
                PE(proj(bu, n, su, hT, l0), r=[f"ring/{su}"] + hk(l0, n), w=[pk_(bu)])
                PE(proj(bg, n, sg_, hT, l0), r=[f"ring/{sg_}"] + hk(l0, n), w=[pk_(bg)])
                def _f():
                    if kind == "p":
                        for j in range(n // 128):
                            mm(PB(bs)[:, j * 128:(j + 1) * 128], lhsT=ones_b[0:2, :], rhs=bhi[0:2, h, :], start=(j == 0), stop=False, skip_group_check=True)
                        for j in range(n // 128):
                            sj = (l0 + j * 128) // 128
                            i = mm(PB(bs)[:, j * 128:(j + 1) * 128], lhsT=zb[:, sj, c * 128:(c + 1) * 128], rhs=WT[:, h, :],
                                   start=False, stop=True, skip_group_check=True)
                    else:
                        mm(PB(bs)[:, 0:n], lhsT=ones_b[0:2, :], rhs=bshi[0:2, h, :], start=True, stop=False)
                        i = mm(PB(bs)[:, 0:n], lhsT=zb[0:64, 8, c * 128:(c + 1) * 128], rhs=WTs[:, h, :], start=False, stop=True)
                    return i
                zkeys = [f"R1A/z{si}/{c // 4}" for si, (a, m) in enumerate(subt) if a < l0 + n and a + m > l0]
                PE(_f, r=["ones_b", "bias", "bias2", "WT"] + (["WTs"] if kind == "s" else []) + zkeys, w=[pk_(bs)])
                sgt = S[:, 4608 + gp * 512:4608 + gp * 512 + n]
                pu = S[:, 5632 + gp * 512:5632 + gp * 512 + n]
                ACT(lambda: act(out=sgt, in_=PB(bg)[:, 0:n], func=AF.Tanh, scale=0.5), w=[pk_(bg), f"SB/sgt{gp}"])
                DVE(lambda: stt(out=pu, in0=sgt, scalar=1.0, in1=PB(bg)[:, 0:n], op0=ALU.add, op1=ALU.mult), r=[f"SB/sgt{gp}"],
                    w=[pk_(bg), f"SB/pu{gp}"])
                DVE(lambda: tt(out=pu, in0=pu, in1=PB(bu)[:, 0:n], op=ALU.mult), w=[pk_(bu), f"SB/pu{gp}"])
                DVE(lambda: tt(out=ysg[:, c, l0:l0 + n], in0=pu, in1=PB(bs)[:, 0:n], op=ALU.mult), r=[f"SB/pu{gp}"],
                    w=[pk_(bs), f"ysg/{c}/{gi}"])
                if gi == len(groups) - 1:
                    ring_after_use()

            for j in range(len(items) + 1):
                if j < len(items):
                    b_s1(j)
                if j >= 2:
                    cj2, gj2 = items[j - 2]
                    if gj2 == len(groups) - 1:
                        b_e1(cj2)
                if j >= 1:
                    a_item(j - 1)
                    b_s2(j - 1)
                    cj, gj = items[j - 1]
                    if gj == 0 and cj >= 1:
                        b_e2(cj - 1)
            b_e1(KC - 1)
            b_e2(KC - 1)
            ring_after_use(NRING)
            ylk = lambda c, kind: [f"ylru/{c}"] if kind == "p" else [f"ylru/s{c}"]
            T.alias("R2D", ["R2B"])
            wo = bfview(R2, 0, [KC, D])
            wgt = bfview(R2, 4096, [KC, D])
            wpl = bfview(R2, 8192, [2, D])

            dw_pieces = [("d_wo", wo[:, k, :], w_out[k * 128:(k + 1) * 128, :], f"R2D/wo/{k}") for k in range(KC)]
            dw_pieces += [("d_wg", wgt[:, k, :], w_pg[k * 128:(k + 1) * 128, :], f"R2D/wg/{k}") for k in range(KC)]
            dw_pieces += [("d_wp", wpl, w_ple.rearrange("(k p) n -> p k n", p=128), "R2D/wp")]
            dw_state = dict(i=0)

            def load_d_piece(n_=1):
                for _ in range(n_):
                    if dw_state["i"] < len(dw_pieces):
                        sem_, dst_, src_, key_ = dw_pieces[dw_state["i"]]
                        T.dma("pool", sem_, dst_, src_, writes=[key_])
                        dw_state["i"] += 1
            WO_KEYS = [f"R2D/wo/{k}" for k in range(KC)]
            WG_KEYS = [f"R2D/wg/{k}" for k in range(KC)]

            T.alias("R1C", ["R1A"])
            T.alias("SC", ["SB"])
            mg = bfview(R1, 0, [KC, NTL])
            for c in range(KC):
                sma = ring_get(); smb = ring_get(); ssg = ring_get(); slr = ring_get()
                cinfo = {}

                def c_gates(gi):
                    l0, n, kind = groups[gi]
                    gp = gi % 2
                    ba_ = next_bank(); bb_ = next_bank()
                    cinfo[gi] = (ba_, bb_)
                    PE(proj(ba_, n, sma, hT, l0), r=[f"ring/{sma}"] + hk(l0, n), w=[pk_(ba_)])
                    PE(proj(bb_, n, smb, hT, l0), r=[f"ring/{smb}"] + hk(l0, n), w=[pk_(bb_)])
                    ta = S[:, gp * 512:gp * 512 + n]
                    tb_ = S[:, 1024 + gp * 512:1024 + gp * 512 + n]
                    ACT(lambda: act(out=ta, in_=PB(ba_)[:, 0:n], func=AF.Tanh, scale=0.5, bias=HBMA(c)), r=["hb"], w=[pk_(ba_), f"SC/ta{gp}"])
                    ACT(lambda: act(out=tb_, in_=PB(bb_)[:, 0:n], func=AF.Tanh, scale=0.5, bias=HBMB(c)), r=["hb"], w=[pk_(bb_), f"SC/tb{gp}"])

                def c_branches(gi):
                    l0, n, kind = groups[gi]
                    gp = gi % 2
                    bA = next_bank(); bB = next_bank()
                    PE(proj(bA, n, ssg, ysg, l0), r=[f"ring/{ssg}"] + [f"ysg/{k}/{gi}" for k in range(KC)], w=[pk_(bA)])
                    PE(proj(bB, n, slr, ylru, l0), r=[f"ring/{slr}"] + sum([ylk(k, kind) for k in range(KC)], []), w=[pk_(bB)])
                    ta = S[:, gp * 512:gp * 512 + n]
                    tb_ = S[:, 1024 + gp * 512:1024 + gp * 512 + n]
                    t1 = S[:, 2048 + gp * 512:2048 + gp * 512 + n]
                    t2 = S[:, 3072 + gp * 512:3072 + gp * 512 + n]
                    DVE(lambda: stt(out=t1, in0=ta, scalar=1.0, in1=PB(bA)[:, 0:n], op0=ALU.add, op1=ALU.mult), r=[f"SC/ta{gp}"],
                        w=[pk_(bA), f"SC/t1{gp}"])
                    DVE(lambda: stt(out=t2, in0=tb_, scalar=1.0, in1=PB(bB)[:, 0:n], op0=ALU.add, op1=ALU.mult), r=[f"SC/tb{gp}"],
                        w=[pk_(bB), f"SC/t2{gp}"])
                    DVE(lambda: tt(out=mg[:, c, l0:l0 + n], in0=t2, in1=t1, op=ALU.add),
                        r=[f"SC/t1{gp}", f"SC/t2{gp}"], w=[f"R1C/mg/{c}/{gi}"])

                if c == 0:
                    c_gates(0); c_gates(1); c_branches(0); c_branches(1)
                    for gi in range(2, len(groups)):
                        c_gates(gi); c_branches(gi)
                else:
                    for gi in range(len(groups)):
                        c_gates(gi); c_branches(gi)
                ring_after_use(NRING)
                if c >= 2:
                    load_d_piece(3)

            load_d_piece(len(dw_pieces))
            st["nbanks"] = 6
            st["bank"] = 0
            T.alias("SD", ["SC"])
            x1v = lambda par: S[:, par * 1024:(par + 1) * 1024]
            x1bv = lambda par: bfview(S, 2048 + par * 512, [1024])
            x1Tv = lambda par: bfview(S, 3072 + par * 512, [KC, 128])
            dinfo = {}

            def d_xload(si):
                l0, n = subt[si]
                sl = si % 2
                T.dma("sp", f"d_x{sl}", xt[0:n, sl, :], xin[row0 + l0:row0 + l0 + n, :], writes=[f"xt/{sl}"])

            def d_a1_pe(si, l0, n):
                par = si % 2
                pr = par
                gi = [i for i, g in enumerate(groups) if g[0] <= l0 < g[0] + g[1]][0]
                def _f():
                    for hf in range(2):
                        for k in range(KC):
                            i = mm(psA[0:n, 2 * pr + hf, :], lhsT=mg[:, k, l0:l0 + n], rhs=wo[:, k, hf * 512:(hf + 1) * 512],
                                   start=(k == 0), stop=(k == KC - 1))
                    return i
                pk = [f"ps/{2 * pr}", f"ps/{2 * pr + 1}"]
                PE(_f, r=WO_KEYS + [f"R1C/mg/{k}/{gi}" for k in range(KC)], w=pk)

            def d_a1_rest(si, l0, n):
                par = si % 2
                pr = par
                c0 = 12 + par * 4
                pk = [f"ps/{2 * pr}", f"ps/{2 * pr + 1}"]
                o2d = psA[0:n, 2 * pr:2 * pr + 2, :].rearrange("p a b -> p (a b)")
                ACT(lambda: act(out=junk[0:n, :], in_=o2d, func=AF.Square, accum_out=smalls[0:n, c0:c0 + 1]), w=pk + ["junk", f"sm/{c0}"])
                DVE(lambda: tsc(out=smalls[0:n, c0 + 1:c0 + 2], in0=smalls[0:n, c0:c0 + 1], scalar1=1.0 / D, scalar2=16.0 * EPS, op0=ALU.mult, op1=ALU.add),
                    r=[f"sm/{c0}"], w=[f"sm/{c0 + 1}"])
                POOL(lambda: GE.tensor_tensor(out=smalls[0:n, c0 + 2:c0 + 3], in0=smalls[0:n, c0 + 1:c0 + 2], in1=nhalf[0:n, 0:1], op=ALU.pow),
                     r=[f"sm/{c0 + 1}", "nhalf"], w=[f"sm/{c0 + 2}"])

            def d_a2(si, l0, n):
                sl = si % 2
                par = si % 2
                pr = par
                c0 = 12 + par * 4
                pk = [f"ps/{2 * pr}", f"ps/{2 * pr + 1}"]
                o2d = psA[0:n, 2 * pr:2 * pr + 2, :].rearrange("p a b -> p (a b)")
                x1 = x1v(par)
                DVE(lambda: stt(out=x1[0:n, :], in0=o2d, scalar=smalls[0:n, c0 + 2:c0 + 3], in1=bc[0:n, 2, :], op0=ALU.mult, op1=ALU.mult),
                    r=[f"sm/{c0 + 2}", "bc"], w=pk + [f"SD/x1{par}"])
                DVE(lambda: tt(out=x1[0:n, :], in0=x1[0:n, :], in1=xt[0:n, sl, :], op=ALU.add), r=[f"xt/{sl}"], w=[f"SD/x1{par}"])
                DVE(lambda: VE.tensor_copy(out=x1bv(par)[0:n, :], in_=x1[0:n, :]), r=[f"SD/x1{par}"], w=[f"SD/x1b{par}"])

            def d_a2_act(si, l0, n):
                par = si % 2
                if si + 2 < len(subt):
                    d_xload(si + 2)

            def d_b1(si, l0, n):
                par = si % 2
                x1, x1b, x1T = x1v(par), x1bv(par), x1Tv(par)
                tb = 0
                pe_ps = psT[:, 1].rearrange("p k t -> p (k t)").bitcast(F32)
                def _f():
                    for k in range(KC):
                        i = TE.transpose(psT[:, tb, k, 0:n], x1b[0:n, k * 128:(k + 1) * 128], ident_b[0:n, 0:n])
                    return i
                PE(_f, r=[f"SD/x1b{par}", "ident_b"], w=[f"psT/{tb}"])
                ACT(lambda: act(out=x1T[:, :, 0:n], in_=psT[:, tb, :, 0:n], func=AF.Copy), w=[f"psT/{tb}", f"SD/x1T{par}"])

            def d_b2(si, l0, n):
                par = si % 2
                x1, x1b, x1T = x1v(par), x1bv(par), x1Tv(par)
                pe_ps = psT[:, 1].rearrange("p k t -> p (k t)").bitcast(F32)
                yv = S[:, 5120 + par * 1024:6144 + par * 1024]
                for hf in range(2):
                    bgt = 4 + hf
                    cs = slice(hf * 512, (hf + 1) * 512)
                    def _f():
                        mm(psA[0:n, bgt, :], lhsT=ones_b[0:2, 0:n], rhs=bghi[0:2, cs], start=True, stop=False)
                        for k in range(KC):
                            i = mm(psA[0:n, bgt, :], lhsT=x1T[:, k, 0:n], rhs=wgt[:, k, cs], start=False, stop=(k == KC - 1))
                        return i
                    PE(_f, r=[f"SD/x1T{par}", "ones_b", "bias", "bias2"] + WG_KEYS, w=[f"ps/{bgt}"])
                    def _f():
                        for j in range(2):
                            i = mm(pe_ps[0:n, :], lhsT=pT[:, j, l0:l0 + n], rhs=wpl[:, j, cs], start=(j == 0), stop=(j == 1))
                        return i
                    PE(_f, r=[f"pT/{si}", "R2D/wp"], w=["psT/1"])
                    tgt = S[:, 4096 + hf * 512:4608 + hf * 512]
                    ACT(lambda: act(out=tgt[0:n, :], in_=psA[0:n, bgt, :], func=AF.Tanh, scale=0.5), w=[f"ps/{bgt}", f"SD/tgt{hf}"])
                    DVE(lambda: stt(out=tgt[0:n, :], in0=tgt[0:n, :], scalar=1.0, in1=pe_ps[0:n, :], op0=ALU.add, op1=ALU.mult),
                        w=["psT/1", f"SD/tgt{hf}"])
                    DVE(lambda: stt(out=yv[0:n, cs], in0=tgt[0:n, :], scalar=0.5, in1=x1[0:n, cs], op0=ALU.mult, op1=ALU.add),
                        r=[f"SD/tgt{hf}", f"SD/x1{par}"], w=[f"SD/y{par}/{hf}"])
                T.dma("sp", f"d_y{par}", yout[row0 + l0:row0 + l0 + n, :], yv[0:n, :], reads=[f"SD/y{par}/0", f"SD/y{par}/1"])

            d_xload(0)
            d_xload(1)
            if pi + 1 < len(PASSES):
                nrow0 = PASSES[pi + 1]["row0"]
                nl0, nn = PASSES[pi + 1]["subt"][0]
                T.dma("sp", "d_x2", xt[0:nn, 2, :], xin[nrow0 + nl0:nrow0 + nl0 + nn, :], writes=["xt/2"])
                T.dma("pool", "d_p0", pb[0:nn, 0, :], pin[nrow0 + nl0:nrow0 + nl0 + nn, :], writes=["pb/0"])
            ns_ = len(subt)
            d_a1_pe(0, *subt[0])
            d_a1_rest(0, *subt[0])
            def d_warm(nd):
                def _f():
                    for _ in range(nd):
                        i = mm(psA[:, 5, :], lhsT=ident_b[:], rhs=wo[:, 0, 0:512], start=True, stop=True)
                    return i
                PE(_f, r=["ident_b", "R2D/wo/0"], w=["ps/5"])

            d_a2(0, *subt[0])
            d_a2_act(0, *subt[0])
            d_a1_pe(1, *subt[1])
            for j in range(ns_):
                d_b1(j, *subt[j])
                if j + 1 < ns_:
                    if j + 1 != 1:
                        d_a1_pe(j + 1, *subt[j + 1])
                    d_a1_rest(j + 1, *subt[j + 1])
                    d_a2(j + 1, *subt[j + 1])
                else:
                    d_warm(6)
                d_b2(j, *subt[j])
                if j + 1 < ns_:
                    d_a2_act(j + 1, *subt[j + 1])

        pr = next_pair()
        def _f():
            for c in range(KC):
                i = TE.transpose(psA[0:4, 2 * pr + c // 4, (c % 4) * 128:(c % 4 + 1) * 128], fin_p[:, c, :], ident_f[:])
            return i
        pk = [f"ps/{2 * pr}", f"ps/{2 * pr + 1}"]
        PE(_f, r=["fin_p", "ident_f"], w=pk)
        fin_pr = xt[0:4, 0, :]
        fin_sr = xt[0:64, 1, :]
        DVE(lambda: VE.tensor_copy(out=fin_pr, in_=psA[0:4, 2 * pr:2 * pr + 2, :].rearrange("p a b -> p (a b)")), w=pk + ["xt/0"])
        T.dma("sp", "d_fp", finp_o, fin_pr, reads=["xt/0"])
        pr = next_pair()
        def _f():
            for c in range(KC):
                i = TE.transpose(psA[0:64, 2 * pr + c // 4, (c % 4) * 128:(c % 4 + 1) * 128], fin_s[:, c, :], ident_f[:])
            return i
        pk = [f"ps/{2 * pr}", f"ps/{2 * pr + 1}"]
        PE(_f, r=["fin_s", "ident_f"], w=pk)
        DVE(lambda: VE.tensor_copy(out=fin_sr, in_=psA[0:64, 2 * pr:2 * pr + 2, :].rearrange("p a b -> p (a b)")), w=pk + ["xt/1"])
        T.dma("sp", "d_fs", fins_o, fin_sr, reads=["xt/1"])

        for name in ("d_y0", "d_y1", "d_cv", "d_fp", "d_fs"):
            T.wait("sp", (name, T.cnt[name]))
    return nc


def kernel(x_prompt, x_sample, p_prompt, p_sample, state_conv, state_lru,
           norm_pre, w_in, sg_norm, sg_w, sg_b, conv_w, conv_b,
           lru_wa, lru_ba, lru_wx, lru_bx, lru_lambda, w_branch_sg, w_branch_lru,
           w_merge, b_merge, w_out, norm_post, w_ple, w_ple_gate, b_ple_gate):
    f = lambda a: np.ascontiguousarray(np.asarray(a, dtype=np.float32))
    x_prompt, x_sample, p_prompt, p_sample = f(x_prompt), f(x_sample), f(p_prompt), f(p_sample)
    state_conv, state_lru = f(state_conv), f(state_lru)
    shared = {
        "norm_pre": f(norm_pre).reshape(1, D), "w_in": f(w_in)[0], "sg_norm": f(sg_norm).reshape(1, D),
        "sg_w": f(sg_w)[0], "sg_b": f(sg_b).reshape(1, 512), "conv_w": f(conv_w)[0], "conv_b": f(conv_b).reshape(1, D),
        "lru_wa": f(lru_wa)[0], "lru_ba": f(lru_ba).reshape(1, D), "lru_wx": f(lru_wx)[0], "lru_bx": f(lru_bx).reshape(1, D),
        "lru_lambda": f(lru_lambda).reshape(1, D), "w_branch_sg": f(w_branch_sg)[0], "w_branch_lru": f(w_branch_lru)[0],
        "w_merge": f(w_merge)[0], "b_merge": f(b_merge).reshape(1, 2 * D), "w_out": f(w_out)[0],
        "norm_post": f(norm_post).reshape(1, D), "w_ple": f(w_ple)[0], "w_ple_gate": f(w_ple_gate)[0],
        "b_ple_gate": f(b_ple_gate).reshape(1, D),
    }
    in_maps = []
    for i in range(NCORES):
        sq = slice(NSQ * i, NSQ * (i + 1))
        m = dict(shared)
        m["xin"] = np.concatenate([x_prompt[i], x_sample[sq].reshape(TSM, D)], axis=0)
        m["pin"] = np.concatenate([p_prompt[0, i], p_sample[0, sq].reshape(TSM, PLE)], axis=0)
        m["sconv"] = state_conv[0, sq].reshape(NSQ * 3, D)
        m["slru"] = state_lru[0, sq]
        in_maps.append(m)
    nc = build_nc()
    res = run_bass_kernel_spmd(nc, in_maps, core_ids=list(range(NCORES)))
    R = res.results
    y_prompt = np.stack([R[i]["yout"][:TPR] for i in range(NCORES)], 0)
    y_sample = np.concatenate([R[i]["yout"][TPR:].reshape(NSQ, 4, D) for i in range(NCORES)], 0)
    conv_prompt = np.stack([R[i]["finp"][0:3] for i in range(NCORES)], 0)[None]
    lru_prompt = np.stack([R[i]["finp"][3] for i in range(NCORES)], 0)[None]
    conv_sample = np.concatenate([R[i]["fins"][0:48].reshape(NSQ, 3, D) for i in range(NCORES)], 0)[None]
    lru_sample = np.concatenate([R[i]["fins"][48:64] for i in range(NCORES)], 0)[None]
    chunk_v = np.concatenate([R[i]["chunkv"].reshape(NSQ, 4, D) for i in range(NCORES)], 0)[None]
    o = lambda a: np.ascontiguousarray(a, dtype=np.float32)
    return (o(y_prompt), o(y_sample), o(conv_prompt), o(lru_prompt), o(conv_sample), o(lru_sample), o(chunk_v))
```
